# Optimizing a Trainium2 kernel written in Bass

```python
import math
import jax, jax.numpy as jnp
from jax import lax
import numpy as np

D_MODEL = 1024
BATCH = 8
SEQ = 2048
DEPTH = 4
DEC_BATCH = 128
DEC_SEQ = 1
PAST_LEN = 16384
PAGE_SIZE = 128

N_MIXERS = 2
N_A_LAYERS = (DEPTH + 1) // 2
N_B_LAYERS = DEPTH // 2
CHUNK = 128
D_A = 2 * D_MODEL
A_HEADS = 8
A_HEAD_DIM = D_A // A_HEADS
GROUP = 16
N_GROUPS = D_MODEL // GROUP
STATE = 64
LOG_DT_MIN = math.log(1e-3)
LOG_DT_MAX = math.log(1e-1)
N_MEM = 256
X_HEADS = 4
X_HEAD_DIM = D_MODEL // X_HEADS
D_FF = 2816
EPS = 1e-6

kernel_name = 'hybrid_gmlp_s5_macaron_memxattn_step'


def _rmsnorm(x, g):
    xf = x.astype(jnp.float32)
    y = xf * lax.rsqrt(jnp.mean(xf * xf, axis=-1, keepdims=True) + EPS)
    return (y * g.astype(jnp.float32)).astype(x.dtype)


def _layernorm(x, g, b):
    xf = x.astype(jnp.float32)
    mu = jnp.mean(xf, axis=-1, keepdims=True)
    xc = xf - mu
    var = jnp.mean(xc * xc, axis=-1, keepdims=True)
    y = xc * lax.rsqrt(var + EPS) * g.astype(jnp.float32) + b.astype(jnp.float32)
    return y.astype(x.dtype)


def _swiglu(h, wg, wu, wd):
    return (jax.nn.silu(h @ wg) * (h @ wu)) @ wd


def _cmul(ar, ai, br, bi):
    return ar * br - ai * bi, ar * bi + ai * br


def _chunk_mlp(h, w_in, ln_g, ln_b, w_s, b_s, w_out):
    bsz, length, _ = h.shape
    u, v = jnp.split(jax.nn.gelu(h @ w_in), 2, axis=-1)
    v = _layernorm(v, ln_g, ln_b)
    blk = length if length <= CHUNK else CHUNK
    n_chunks = -(-length // blk)
    pad = n_chunks * blk - length
    vc = jnp.pad(v, ((0, 0), (0, pad), (0, 0))).reshape(bsz, n_chunks, blk, A_HEADS, A_HEAD_DIM)
    ws = jnp.tril(w_s[:, :blk, :blk])
    bias = jnp.swapaxes(b_s[:, :blk], 0, 1)[:, :, None]
    mixed = jnp.einsum('hts,bcshd->bcthd', ws, vc) + bias
    mixed = mixed.reshape(bsz, n_chunks * blk, D_A)[:, :length]
    return (u * mixed) @ w_out, v


def _ssm_mixer(h, h0_re, h0_im, a_re, a_im, log_dt, b_re, b_im, c_re, c_im, d_skip, w_glu):
    f32 = jnp.float32
    bsz, length, _ = h.shape
    u = h.astype(f32).reshape(bsz, length, N_GROUPS, GROUP)
    a_re = a_re.astype(f32)
    a_im = a_im.astype(f32)
    dt = jnp.exp(log_dt.astype(f32))[:, None]
    mag = jnp.exp(a_re * dt)
    ab_re = mag * jnp.cos(a_im * dt)
    ab_im = mag * jnp.sin(a_im * dt)
    den = a_re * a_re + a_im * a_im
    nr = ab_re - 1.0
    ni = ab_im
    f_re = (nr * a_re + ni * a_im) / den
    f_im = (ni * a_re - nr * a_im) / den
    bb_re, bb_im = _cmul(f_re[..., None], f_im[..., None], b_re.astype(f32), b_im.astype(f32))
    bu_re = jnp.einsum('gpc,blgc->blgp', bb_re, u)
    bu_im = jnp.einsum('gpc,blgc->blgp', bb_im, u)
    shape = (1, length, N_GROUPS, STATE)
    elems = (jnp.broadcast_to(ab_re, shape), jnp.broadcast_to(ab_im, shape), bu_re, bu_im)

    def combine(e1, e2):
        a1r, a1i, b1r, b1i = e1
        a2r, a2i, b2r, b2i = e2
        ar, ai = _cmul(a2r, a2i, a1r, a1i)
        br, bi = _cmul(a2r, a2i, b1r, b1i)
        return ar, ai, br + b2r, bi + b2i

    acr, aci, bcr, bci = lax.associative_scan(combine, elems, axis=1)
    pr, pi = _cmul(acr, aci, h0_re.astype(f32)[:, None], h0_im.astype(f32)[:, None])
    hr = pr + bcr
    hi = pi + bci
    c_re = c_re.astype(f32)
    c_im = c_im.astype(f32)
    y = (jnp.einsum('gcp,blgp->blgc', c_re, hr) - jnp.einsum('gcp,blgp->blgc', c_im, hi)
         + d_skip.astype(f32) * u)
    z = jax.nn.gelu(y.reshape(bsz, length, D_MODEL)).astype(h.dtype)
    val, gate = jnp.split(z @ w_glu, 2, axis=-1)
    return val * jax.nn.sigmoid(gate), hr[:, -1], hi[:, -1]


def _mem_kv(mem, g, w_k, w_v):
    bsz = mem.shape[0]
    m = _rmsnorm(mem, g)
    k = (m @ w_k).reshape(bsz, N_MEM, X_HEADS, X_HEAD_DIM)
    v = (m @ w_v).reshape(bsz, N_MEM, X_HEADS, X_HEAD_DIM)
    return k, v


def _cross_attn(h, k, v, w_q, w_o):
    bsz, length, _ = h.shape
    q = (h @ w_q).reshape(bsz, length, X_HEADS, X_HEAD_DIM)
    s = jnp.einsum('blhd,bmhd->bhlm', q, k).astype(jnp.float32) * (X_HEAD_DIM ** -0.5)
    p = jax.nn.softmax(s, axis=-1).astype(v.dtype)
    o = jnp.einsum('bhlm,bmhd->blhd', p, v).reshape(bsz, length, D_MODEL)
    return o @ w_o


def _trunk(x, mem_k, mem_v, h0_re, h0_im, w):
    (norm_ffn1, ffn1_wg, ffn1_wu, ffn1_wd, norm_mix,
     a_w_in, a_ln_g, a_ln_b, a_w_s, a_b_s, a_w_out,
     b_a_re, b_a_im, b_log_dt, b_b_re, b_b_im, b_c_re, b_c_im, b_d, b_w_glu,
     norm_x, x_wq, x_wo,
     norm_ffn2, ffn2_wg, ffn2_wu, ffn2_wd, norm_final) = w
    new_re, new_im, new_v = [], [], []
    for i in range(DEPTH):
        x = x + 0.5 * _swiglu(_rmsnorm(x, norm_ffn1[i]), ffn1_wg[i], ffn1_wu[i], ffn1_wd[i])
        h = _rmsnorm(x, norm_mix[i])
        j = i // N_MIXERS
        if i % N_MIXERS == 0:
            out, v_rows = _chunk_mlp(h, a_w_in[j], a_ln_g[j], a_ln_b[j], a_w_s[j], a_b_s[j], a_w_out[j])
            new_v.append(v_rows)
        else:
            out, hr, hi = _ssm_mixer(h, h0_re[j], h0_im[j], b_a_re[j], b_a_im[j], b_log_dt[j],
                                     b_b_re[j], b_b_im[j], b_c_re[j], b_c_im[j], b_d[j], b_w_glu[j])
            new_re.append(hr)
            new_im.append(hi)
        x = x + out
        x = x + _cross_attn(_rmsnorm(x, norm_x[i]), mem_k[i], mem_v[i], x_wq[i], x_wo[i])
        x = x + 0.5 * _swiglu(_rmsnorm(x, norm_ffn2[i]), ffn2_wg[i], ffn2_wu[i], ffn2_wd[i])
    return _rmsnorm(x, norm_final), jnp.stack(new_re), jnp.stack(new_im), jnp.stack(new_v)


def setup_inputs(seed: int = 0) -> dict:
    key = jax.random.key(seed)
    ks = iter(jax.random.split(key, 64))
    f32 = jnp.float32

    def nrm(shape, scale):
        return jax.random.normal(next(ks), shape, f32) * scale

    def gain(shape):
        return 1.0 + nrm(shape, 0.05)

    d = D_MODEL
    inp = {}
    inp['x_prompt'] = nrm((BATCH, SEQ, d), 1.0)
    inp['x_sample'] = nrm((DEC_BATCH, DEC_SEQ, d), 1.0)
    inp['cache_mem_k'] = nrm((DEPTH, DEC_BATCH, N_MEM, X_HEADS, X_HEAD_DIM), 1.0)
    inp['cache_mem_v'] = nrm((DEPTH, DEC_BATCH, N_MEM, X_HEADS, X_HEAD_DIM), 1.0)
    inp['state_ssm_re'] = nrm((N_B_LAYERS, DEC_BATCH, N_GROUPS, STATE), 0.5)
    inp['state_ssm_im'] = nrm((N_B_LAYERS, DEC_BATCH, N_GROUPS, STATE), 0.5)
    inp['mem_prompt'] = nrm((BATCH, N_MEM, d), 1.0)
    inp['norm_ffn1'] = gain((DEPTH, d))
    inp['ffn1_wg'] = nrm((DEPTH, d, D_FF), d ** -0.5)
    inp['ffn1_wu'] = nrm((DEPTH, d, D_FF), d ** -0.5)
    inp['ffn1_wd'] = nrm((DEPTH, D_FF, d), D_FF ** -0.5)
    inp['norm_mix'] = gain((DEPTH, d))
    inp['a_w_in'] = nrm((N_A_LAYERS, d, 2 * D_A), d ** -0.5)
    inp['a_ln_g'] = gain((N_A_LAYERS, D_A))
    inp['a_ln_b'] = nrm((N_A_LAYERS, D_A), 0.02)
    inp['a_w_s'] = nrm((N_A_LAYERS, A_HEADS, CHUNK, CHUNK), CHUNK ** -0.5)
    inp['a_b_s'] = 1.0 + nrm((N_A_LAYERS, A_HEADS, CHUNK), 0.1)
    inp['a_w_out'] = nrm((N_A_LAYERS, D_A, d), D_A ** -0.5)
    n_idx = jnp.arange(STATE, dtype=f32)
    inp['b_a_re'] = -0.5 + nrm((N_B_LAYERS, N_GROUPS, STATE), 0.01)
    inp['b_a_im'] = math.pi * n_idx + nrm((N_B_LAYERS, N_GROUPS, STATE), 0.01)
    inp['b_log_dt'] = jax.random.uniform(next(ks), (N_B_LAYERS, N_GROUPS), f32, LOG_DT_MIN, LOG_DT_MAX)
    inp['b_b_re'] = nrm((N_B_LAYERS, N_GROUPS, STATE, GROUP), (2.0 * GROUP) ** -0.5)
    inp['b_b_im'] = nrm((N_B_LAYERS, N_GROUPS, STATE, GROUP), (2.0 * GROUP) ** -0.5)
    inp['b_c_re'] = nrm((N_B_LAYERS, N_GROUPS, GROUP, STATE), (2.0 * STATE) ** -0.5)
    inp['b_c_im'] = nrm((N_B_LAYERS, N_GROUPS, GROUP, STATE), (2.0 * STATE) ** -0.5)
    inp['b_d'] = nrm((N_B_LAYERS, N_GROUPS, GROUP), 1.0)
    inp['b_w_glu'] = nrm((N_B_LAYERS, d, 2 * d), d ** -0.5)
    inp['norm_x'] = gain((DEPTH, d))
    inp['norm_mem'] = gain((DEPTH, d))
    inp['x_wq'] = nrm((DEPTH, d, d), d ** -0.5)
    inp['x_wk'] = nrm((DEPTH, d, d), d ** -0.5)
    inp['x_wv'] = nrm((DEPTH, d, d), d ** -0.5)
    inp['x_wo'] = nrm((DEPTH, d, d), d ** -0.5)
    inp['norm_ffn2'] = gain((DEPTH, d))
    inp['ffn2_wg'] = nrm((DEPTH, d, D_FF), d ** -0.5)
    inp['ffn2_wu'] = nrm((DEPTH, d, D_FF), d ** -0.5)
    inp['ffn2_wd'] = nrm((DEPTH, D_FF, d), D_FF ** -0.5)
    inp['norm_final'] = gain((d,))
    return inp


def reference(x_prompt, x_sample, cache_mem_k, cache_mem_v, state_ssm_re, state_ssm_im, mem_prompt,
              norm_ffn1, ffn1_wg, ffn1_wu, ffn1_wd, norm_mix,
              a_w_in, a_ln_g, a_ln_b, a_w_s, a_b_s, a_w_out,
              b_a_re, b_a_im, b_log_dt, b_b_re, b_b_im, b_c_re, b_c_im, b_d, b_w_glu,
              norm_x, norm_mem, x_wq, x_wk, x_wv, x_wo,
              norm_ffn2, ffn2_wg, ffn2_wu, ffn2_wd, norm_final):
    w = (norm_ffn1, ffn1_wg, ffn1_wu, ffn1_wd, norm_mix,
         a_w_in, a_ln_g, a_ln_b, a_w_s, a_b_s, a_w_out,
         b_a_re, b_a_im, b_log_dt, b_b_re, b_b_im, b_c_re, b_c_im, b_d, b_w_glu,
         norm_x, x_wq, x_wo,
         norm_ffn2, ffn2_wg, ffn2_wu, ffn2_wd, norm_final)

    kv = [_mem_kv(mem_prompt, norm_mem[i], x_wk[i], x_wv[i]) for i in range(DEPTH)]
    mem_k_prompt = jnp.stack([k for k, _ in kv])
    mem_v_prompt = jnp.stack([v for _, v in kv])
    h0 = jnp.zeros((N_B_LAYERS, x_prompt.shape[0], N_GROUPS, STATE), jnp.float32)
    y_prompt, ssm_re_prompt, ssm_im_prompt, _ = _trunk(x_prompt, mem_k_prompt, mem_v_prompt, h0, h0, w)

    y_sample, ssm_re_sample, ssm_im_sample, chunk_v_sample = _trunk(
        x_sample, cache_mem_k, cache_mem_v, state_ssm_re, state_ssm_im, w)

    return (y_prompt, y_sample, mem_k_prompt, mem_v_prompt, ssm_re_prompt, ssm_im_prompt,
            ssm_re_sample, ssm_im_sample, chunk_v_sample)
```

```python
import numpy as np
import concourse.bass as bass
import concourse.mybir as mybir
from concourse.bass_utils import run_bass_kernel_spmd

F32 = mybir.dt.float32
BF16 = mybir.dt.bfloat16
AF = mybir.ActivationFunctionType
ALU = mybir.AluOpType
EPS = 1e-6


class Reg:
    __slots__ = ("w", "r", "excl")

    def __init__(self, excl=False):
        self.w = None
        self.r = []
        self.excl = excl


class Prog:
    ENGS = ("pe", "act", "dve", "pool", "sp")
    EPOCH = 30000

    def __init__(self):
        self.streams = {e: [] for e in self.ENGS}
        self.cnt = {e: 0 for e in self.ENGS}
        self.epoch = {e: 0 for e in self.ENGS}
        self.waited = {e: {} for e in self.ENGS}
        self.dma_cnt = {}
        self.semkeys = []

    def _ekey(self, eng):
        k = (eng, self.epoch[eng])
        if k not in self.semkeys:
            self.semkeys.append(k)
        return k

    def _wait(self, eng, tok):
        if tok is None:
            return
        key, val = tok
        if key[0] == eng and eng == "pe":
            return
        if self.waited[eng].get(key, 0) >= val:
            return
        self.waited[eng][key] = val
        self.streams[eng].append(("wait", key, val))

    def op(self, eng, fn, reads=(), writes=(), inc=True):
        for r in reads:
            self._wait(eng, r.w)
            if r.excl:
                for t in r.r:
                    if t is not None and t[0][0] != eng:
                        self._wait(eng, t)
        for w in writes:
            self._wait(eng, w.w)
            for t in w.r:
                self._wait(eng, t)
        key = self._ekey(eng)
        if inc:
            self.cnt[eng] += 1
            tok = (key, self.cnt[eng])
            self.streams[eng].append(("op", fn, key))
            if self.cnt[eng] >= self.EPOCH:
                self.epoch[eng] += 1
                self.cnt[eng] = 0
        else:
            tok = (key, self.cnt[eng] + 1)
            self.streams[eng].append(("op", fn, None))
        for r in reads:
            r.r.append(tok)
        for w in writes:
            w.w = tok
            w.r = []
        return tok

    def dma(self, q, out, in_, semkey, reads=(), writes=()):
        for r in reads:
            self._wait(q, r.w)
        for w in writes:
            self._wait(q, w.w)
            for t in w.r:
                self._wait(q, t)
        key = ("dma", semkey)
        if key not in self.semkeys:
            self.semkeys.append(key)
        self.dma_cnt[key] = self.dma_cnt.get(key, 0) + 16
        tok = (key, self.dma_cnt[key])
        self.streams[q].append(("dma", out, in_, key))
        for r in reads:
            r.r.append(tok)
        for w in writes:
            w.w = tok
            w.r = []
        return tok

    def wait_tok(self, eng, tok):
        self._wait(eng, tok)

    def mm(self, out, lhsT, rhs, start, stop, reads=(), writes=()):
        return self.op("pe", lambda e: e.matmul(out, lhsT, rhs, start=start, stop=stop),
                       reads=reads, writes=writes, inc=stop)

    def act(self, out, in_, func, reads=(), writes=(), eng="act", **kw):
        return self.op(eng, lambda e: e.activation(out=out, in_=in_, func=func, **kw),
                       reads=reads, writes=writes)

    def ts(self, eng, out, in0, s1, s2, op0, op1=None, reads=(), writes=()):
        if op1 is None:
            return self.op(eng, lambda e: e.tensor_scalar(out=out, in0=in0, scalar1=s1, scalar2=None, op0=op0),
                           reads=reads, writes=writes)
        return self.op(eng, lambda e: e.tensor_scalar(out=out, in0=in0, scalar1=s1, scalar2=s2, op0=op0, op1=op1),
                       reads=reads, writes=writes)

    def tt(self, eng, out, in0, in1, op, reads=(), writes=()):
        return self.op(eng, lambda e: e.tensor_tensor(out=out, in0=in0, in1=in1, op=op),
                       reads=reads, writes=writes)

    def stt(self, eng, out, in0, scalar, in1, op0, op1, reads=(), writes=()):
        return self.op(eng, lambda e: e.scalar_tensor_tensor(out=out, in0=in0, scalar=scalar, in1=in1, op0=op0, op1=op1),
                       reads=reads, writes=writes)

    def copy(self, eng, out, in_, reads=(), writes=()):
        if eng == "act":
            return self.op(eng, lambda e: e.copy(out=out, in_=in_), reads=reads, writes=writes)
        return self.op(eng, lambda e: e.tensor_copy(out=out, in_=in_), reads=reads, writes=writes)

    def emit(self, nc, sems):
        engmap = {"pe": "tensor", "act": "scalar", "dve": "vector", "pool": "gpsimd", "sp": "sync"}

        def run(e, stream):
            for item in stream:
                if item[0] == "wait":
                    e.wait_ge(sems[item[1]], item[2])
                elif item[0] == "op":
                    ins = item[1](e)
                    if item[2] is not None:
                        ins.then_inc(sems[item[2]], 1)
                else:
                    e.dma_start(out=item[1], in_=item[2]).then_inc(sems[item[3]], 16)

        with nc.Block() as block:
            for eng in self.ENGS:
                getattr(block, engmap[eng])(lambda e, s=self.streams[eng]: run(e, s))


class Cfg:
    def __init__(self, seq=2048, ns=16, depth=4, dff=2816):
        self.D = 1024
        self.KC = 8
        self.SEQ = seq
        self.NS = ns
        self.T = seq + ns
        self.DEPTH = depth
        self.DFF = dff
        self.FC = dff // 128
        self.NB = seq // 512
        self.blocks = [(i * 512, 512) for i in range(self.NB)] + [(seq, ns)]
        self.NMEM = 256


WSLOT = 2048
NW = 4


class Builder:
    def __init__(self, cfg, nc, dry_pieces=None):
        self.cfg = cfg
        self.nc = nc
        self.P = Prog()
        self.dry = dry_pieces is None
        self.pieces = [] if self.dry else dry_pieces
        self.piece_i = 0
        self.issued = 0
        self.ps_i = 0
        self.tmp_i = {}

    def psum(self):
        i = self.ps_i % 8
        self.ps_i += 1
        return self.ps[i], self.psr[i]

    def wget(self, dram_ap, shape, dep=None):
        k = self.piece_i
        self.piece_i += 1
        if self.dry:
            self.pieces.append((dram_ap, shape, dep))
        else:
            last = min(k + NW - 2, len(self.pieces) - 1)
            while self.issued <= last:
                j = self.issued
                d_ap, shp, dep_ = self.pieces[j]
                if dep_ is not None:
                    for k_, v_ in self.scr_toks[dep_].items():
                        self.P.wait_tok("pool", (k_, v_))
                self.P.dma("pool", self._slot_ap(j % NW, shp), d_ap, ("w", j % NW), writes=[self.wreg[j % NW]])
                self.issued += 1
        return self._slot_ap(k % NW, shape), self.wreg[k % NW]

    def _slot_ap(self, s, shape):
        n = int(np.prod(shape))
        assert n <= WSLOT, (shape, n)
        ap = self.wslot[s][:, 0:n]
        if len(shape) == 2:
            return ap.rearrange("p (a b) -> p a b", a=shape[0])
        if len(shape) == 3:
            return ap.rearrange("p (a b c) -> p a b c", a=shape[0], b=shape[1])
        return ap

    def carve(self, n):
        best = {}
        for r in getattr(self, "arena_regs", []):
            for t in ([r.w] if r.w else []) + list(r.r):
                if t is not None and best.get(t[0], 0) < t[1]:
                    best[t[0]] = t[1]
        inherit = [(k, v) for k, v in best.items()]
        regs = []
        for _ in range(n):
            r = Reg()
            r.r = list(inherit)
            regs.append(r)
        self.arena_regs = regs
        return regs

    def carve2(self, n):
        best = {}
        for r in getattr(self, "s2_regs", []):
            for t in ([r.w] if r.w else []) + list(r.r):
                if t is not None and best.get(t[0], 0) < t[1]:
                    best[t[0]] = t[1]
        inherit = [(k, v) for k, v in best.items()]
        regs = []
        for _ in range(n):
            r = Reg()
            r.r = list(inherit)
            regs.append(r)
        self.s2_regs = regs
        return regs

    def rot(self, name, n):
        i = self.tmp_i.get(name, 0)
        self.tmp_i[name] = i + 1
        return i % n

    def declare(self, stack):
        cfg, nc = self.cfg, self.nc
        L, D, T = cfg.DEPTH, cfg.D, cfg.T
        din = lambda name, shape: nc.dram_tensor(name, list(shape), F32, kind="ExternalInput").ap()
        dout = lambda name, shape: nc.dram_tensor(name, list(shape), F32, kind="ExternalOutput").ap()
        self.d = {}
        self.d["xT"] = din("xT", [D, T])
        self.NA = (L + 1) // 2
        self.NBL = L // 2
        self.NV = (5 * L + 1) * 8 + self.NA * 16
        self.d["vecs"] = din("vecs", [128, self.NV])
        for nm in ("ffn1", "ffn2"):
            self.d[nm + "_wg"] = din(nm + "_wg", [L, D, cfg.DFF])
            self.d[nm + "_wu"] = din(nm + "_wu", [L, D, cfg.DFF])
            self.d[nm + "_wd"] = din(nm + "_wd", [L, cfg.DFF, D])
        self.d["memT"] = din("memT", [D, cfg.NMEM])
        for nm in ("x_wq", "x_wk", "x_wv", "x_wo"):
            self.d[nm] = din(nm, [L, D, D])
        self.d["cacheKT"] = din("cacheKT", [L, cfg.NS, D, cfg.NMEM])
        self.d["cacheV"] = din("cacheV", [L, cfg.NS, cfg.NMEM, D])
        NA = self.NA
        self.d["a_w_in"] = din("a_w_in", [NA, D, 4096])
        self.d["a_w_out"] = din("a_w_out", [NA, 2048, D])
        self.d["a_ln_g"] = din("a_ln_g", [NA, 2048])
        self.d["a_ln_b"] = din("a_ln_b", [NA, 2048])
        self.d["a_wsT"] = din("a_wsT", [NA, 128, 8, 128])
        self.d["a_bs"] = din("a_bs", [NA, 1024])
        self.d["a_w0"] = din("a_w0", [NA, 16])
        self.d["trimask"] = din("trimask", [128, 128])
        self.d["ident16"] = din("ident16", [16, 16])
        self.d["chunkv_out"] = dout("chunkv_out", [NA, cfg.NS, 2048])
        NBL = max(self.NBL, 1)
        self.d["ssm_small"] = din("ssm_small", [NBL, 128, 3, 32])
        self.d["ssm_B"] = din("ssm_B", [NBL, 2, 128, 512])
        self.d["ssm_C"] = din("ssm_C", [NBL, 2, 128, 512])
        self.d["ssm_Dcol"] = din("ssm_Dcol", [NBL, 128, 64])
        self.d["b_w_glu"] = din("b_w_glu", [NBL, D, 2 * D])
        self.d["ssm_h0"] = din("ssm_h0", [NBL, 2, 128, 512])
        self.d["blockmask"] = din("blockmask", [128, 128])
        self.d["ident128"] = din("ident128", [128, 128])
        self.d["kvec"] = din("kvec", [128, 256])
        self.d["nvec"] = din("nvec", [128, 32])
        self.d["selmat"] = din("selmat", [128, 64 * 128])
        bdout = lambda name, shape: nc.dram_tensor(name, list(shape), BF16, kind="ExternalOutput").ap()
        self.d["sc_wkc"] = bdout("sc_wkc", [NBL, 128, 16, 1536])
        self.d["sc_tab"] = dout("sc_tab", [NBL, 128, 32, 2, 256])
        self.d["ssm_pr_out"] = dout("ssm_pr_out", [NBL, 128, 32, 2])
        self.d["ssm_s_out"] = dout("ssm_s_out", [NBL, 2, 128, 512])
        self.d["yT"] = dout("yT", [D, T])
        self.d["memk_out"] = dout("memk_out", [L, D, cfg.NMEM])
        self.d["memv_out"] = dout("memv_out", [L, cfg.NMEM, D])
        self.out_toks = []

        sb = lambda name, shape, dt: stack.enter_context(nc.sbuf_tensor(name, list(shape), dt))
        self.xT = sb("xT_sb", [128, cfg.KC, T], F32)
        self.hT = sb("hT_sb", [128, cfg.KC, T], BF16)
        self.ARENA = max(12 * T, 24768)
        self.arena = sb("arena", [128, self.ARENA], BF16)
        self.wslot = [sb(f"wslot{i}", [128, WSLOT], BF16) for i in range(NW)]
        self.wreg = [Reg() for _ in range(NW)]
        self.vecs = sb("vecs_sb", [128, self.NV], F32)
        self.mrstd = sb("mrstd_sb", [128, cfg.NMEM], F32)
        self.mrstd_r = Reg()
        self.ssmc = sb("ssmc_sb", [128, max(self.NBL, 1), 5, 32], F32)
        self.ssmc_r = Reg()
        self.scr_toks = [dict() for _ in range(max(self.NBL, 1))]
        self.epsc = sb("epsc_sb", [128, 1], F32)
        self.ones = sb("ones_sb", [128, 128], BF16)
        self.sq = [sb(f"sq{i}", [128, cfg.KC, 512], BF16) for i in range(2)]
        self.sqr = [Reg() for _ in range(2)]
        self.rs = [sb(f"rs{i}", [128, 512], F32) for i in range(2)]
        self.rsr = [Reg() for _ in range(2)]
        self.S2N = 8192
        self.s2 = sb("s2", [128, self.S2N], BF16)
        self.ps = [stack.enter_context(nc.psum_tensor(f"ps{i}", [128, 512], F32)) for i in range(8)]
        self.psr = [Reg(excl=True) for _ in range(8)]
        nb = len(cfg.blocks)
        self.xr = [Reg() for _ in range(nb)]
        self.hr = [Reg() for _ in range(nb)]
        self.constr = Reg()

    def vcol(self, kind, layer):
        L = self.cfg.DEPTH
        order = {"ffn1": 0, "mix": 1, "x": 2, "ffn2": 3, "mem": 4}
        if kind == "final":
            return 5 * L * 8
        return (order[kind] * L + layer) * 8

    def prologue(self):
        cfg, P = self.cfg, self.P
        xT_d = self.d["xT"].rearrange("(c p) t -> p c t", p=128)
        for b, (t0, n) in enumerate(cfg.blocks):
            P.dma("sp", self.xT[:, :, t0:t0 + n], xT_d[:, :, t0:t0 + n], ("x", b), writes=[self.xr[b]])
        P.dma("sp", self.vecs[:, :], self.d["vecs"][:, :], ("c", 0), writes=[self.constr])
        P.op("pool", lambda e: e.memset(self.ones[:, :], 1.0), writes=[self.constr])
        P.op("pool", lambda e: e.memset(self.epsc[:, :], EPS), writes=[self.constr])
        (mr,) = self.carve2(1)
        memT = self.s2[:, 0:4096].bitcast(F32).rearrange("p (c m) -> p c m", c=cfg.KC)
        P.dma("sp", memT, self.d["memT"].rearrange("(c p) m -> p c m", p=128), ("c", 1), writes=[mr])
        sq, sqr = self.sq[0], self.sqr[0]
        P.act(sq[:, :, :cfg.NMEM], memT, AF.Square, reads=[mr], writes=[sqr])
        ps, psr = self.psum()
        for c in range(cfg.KC):
            P.mm(ps[:, :cfg.NMEM], self.ones[:, :], sq[:, c, :cfg.NMEM], c == 0, c == cfg.KC - 1,
                 reads=[sqr, self.constr], writes=[psr])
        P.act(self.mrstd[:, :], ps[:, :cfg.NMEM], AF.Sqrt, reads=[psr, self.constr], writes=[self.mrstd_r],
              scale=1.0 / cfg.D, bias=self.epsc[:, 0:1])
        P.op("dve", lambda e: e.reciprocal(out=self.mrstd[:, :], in_=self.mrstd[:, :]), reads=[self.mrstd_r], writes=[self.mrstd_r])

    def rmsnorm(self, gcol, out=None, out_regs=None):
        out = self.hT if out is None else out
        out_regs = self.hr if out_regs is None else out_regs
        self._rmsnorm_blocks(gcol, lambda b, c, t0, n: out[:, c, t0:t0 + n], lambda b: out_regs[b])

    def _rmsnorm_blocks(self, gcol, out_ap, out_reg, after=None, perm=False):
        cfg, P = self.cfg, self.P
        blocks = cfg.blocks
        st = {}

        def stage_a(b):
            t0, n = blocks[b]
            i = self.rot("sq", 2)
            sq, sqr = self.sq[i], self.sqr[i]
            P.act(sq[:, :, :n], self.xT[:, :, t0:t0 + n], AF.Square, reads=[self.xr[b]], writes=[sqr])
            ps, psr = self.psum()
            for c in range(cfg.KC):
                P.mm(ps[:, :n], self.ones[:, :], sq[:, c, :n], c == 0, c == cfg.KC - 1,
                     reads=[sqr, self.constr], writes=[psr])
            st[b] = (ps, psr)

        def stage_b(b):
            t0, n = blocks[b]
            ps, psr = st[b]
            i = self.rot("rs", 2)
            rs, rsr = self.rs[i], self.rsr[i]
            P.act(rs[:, :n], ps[:, :n], AF.Sqrt, reads=[psr, self.constr], writes=[rsr],
                  scale=1.0 / cfg.D, bias=self.epsc[:, 0:1])
            P.op("dve", lambda e: e.reciprocal(out=rs[:, :n], in_=rs[:, :n]), reads=[rsr], writes=[rsr])
            for c in range(cfg.KC):
                if perm and n == 512:
                    o_ = self.hT[:, c, 0:cfg.SEQ].rearrange("p (s k) -> p s k", s=8)[:, :, t0 // 8:(t0 + n) // 8]
                    i0 = self.xT[:, c, t0:t0 + n].rearrange("p (k s) -> p s k", s=8)
                    i1 = rs[:, :n].rearrange("p (k s) -> p s k", s=8)
                    P.stt("dve", o_, i0, self.vecs[:, gcol + c:gcol + c + 1], i1, ALU.mult, ALU.mult,
                          reads=[self.xr[b], rsr, self.constr], writes=list(self.hr))
                    continue
                oap = self.hT[:, c, t0:t0 + n] if out_ap is None else out_ap(b, c, t0, n)
                P.stt("dve", oap, self.xT[:, c, t0:t0 + n], self.vecs[:, gcol + c:gcol + c + 1],
                      rs[:, :n], ALU.mult, ALU.mult, reads=[self.xr[b], rsr, self.constr], writes=[out_reg(b)])
            if after is not None:
                after(b, t0, n)

        nb = len(blocks)
        stage_a(0)
        for b in range(nb):
            if b + 1 < nb:
                stage_a(b + 1)
            stage_b(b)

    def ffn(self, nm, layer, gcol):
        cfg, P = self.cfg, self.P
        self.rmsnorm(gcol)
        wg = self.d[nm + "_wg"][layer].rearrange("(kc p) f -> p kc f", p=128)
        wu = self.d[nm + "_wu"][layer].rearrange("(kc p) f -> p kc f", p=128)
        wd = self.d[nm + "_wd"][layer].rearrange("(kc p) d -> p kc d", p=128)
        npieces = cfg.DFF // 256
        halves = [list(range(0, (npieces + 1) // 2)), list(range((npieces + 1) // 2, npieces))]
        nb = len(cfg.blocks)
        sgr = self.carve2(3)
        sg = [self.s2[:, i * 512:(i + 1) * 512] for i in range(3)]
        for pcs in halves:
            nfc = 2 * len(pcs)
            assert nfc * cfg.T <= self.ARENA
            act = self.arena[:, 0:nfc * cfg.T].rearrange("p (f t) -> p f t", f=nfc)
            flat = self.carve(nfc * nb)
            actr = [[flat[f * nb + b] for b in range(nb)] for f in range(nfc)]
            for pi, pc in enumerate(pcs):
                wgs, wgr = self.wget(wg[:, :, pc * 256:(pc + 1) * 256], [cfg.KC, 256])
                wus, wur = self.wget(wu[:, :, pc * 256:(pc + 1) * 256], [cfg.KC, 256])
                for j in range(2):
                    fl = pi * 2 + j
                    for b, (t0, n) in enumerate(cfg.blocks):
                        pg, pgr = self.psum()
                        pu, pur = self.psum()
                        for c in range(cfg.KC):
                            P.mm(pg[:, :n], wgs[:, c, j * 128:(j + 1) * 128], self.hT[:, c, t0:t0 + n],
                                 c == 0, c == cfg.KC - 1, reads=[wgr, self.hr[b]], writes=[pgr])
                        for c in range(cfg.KC):
                            P.mm(pu[:, :n], wus[:, c, j * 128:(j + 1) * 128], self.hT[:, c, t0:t0 + n],
                                 c == 0, c == cfg.KC - 1, reads=[wur, self.hr[b]], writes=[pur])
                        si = self.rot("sg", 3)
                        P.act(sg[si][:, :n], pg[:, :n], AF.Silu, reads=[pgr], writes=[sgr[si]])
                        P.tt("dve", act[:, fl, t0:t0 + n], sg[si][:, :n], pu[:, :n], ALU.mult,
                             reads=[sgr[si], pur], writes=[actr[fl][b]])
            k0 = pcs[0] * 2
            for dc in range(cfg.KC):
                wds, wdr = self.wget(wd[:, k0:k0 + nfc, dc * 128:(dc + 1) * 128], [nfc, 128])
                for b, (t0, n) in enumerate(cfg.blocks):
                    po, por = self.psum()
                    for fl in range(nfc):
                        P.mm(po[:, :n], wds[:, fl, :], act[:, fl, t0:t0 + n],
                             fl == 0, fl == nfc - 1, reads=[wdr, actr[fl][b]], writes=[por])
                    P.stt("dve", self.xT[:, dc, t0:t0 + n], po[:, :n], 0.5, self.xT[:, dc, t0:t0 + n],
                          ALU.mult, ALU.add, reads=[por, self.xr[b]], writes=[self.xr[b]])


    def ssm_setup(self, jb):
        cfg, P = self.cfg, self.P
        PI = float(np.pi)
        TWO_PI = float(2 * np.pi)
        A32 = self.arena[:, 0:(self.ARENA // 2) * 2].bitcast(F32)
        S32 = self.s2[:, :].bitcast(F32)
        pool_a = [A32, 0, self.ARENA // 2]
        pool_s = [S32, 0, self.S2N // 2]
        (ar,) = self.carve(1)
        (sr,) = self.carve2(1)
        R = Reg()
        R.r = list(ar.r) + list(sr.r)

        def alloc(n, pool=None):
            for pl in ([pool] if pool else [pool_a, pool_s]):
                if pl[1] + n <= pl[2]:
                    a = pl[0][:, pl[1]:pl[1] + n]
                    pl[1] += n
                    return a
            raise AssertionError(f"setup scratch overflow need {n} a={pool_a[1]}/{pool_a[2]} s={pool_s[1]}/{pool_s[2]}")

        def v3(ap, a):
            return ap.rearrange("p (a b) -> p a b", a=a)

        def tt(out, a, b, op, eng="dve"):
            P.tt(eng, out, a, b, op, reads=[R], writes=[R])

        sm = alloc(96)
        P.dma("sp", v3(sm, 3), self.d["ssm_small"][jb], ("ss", 0), writes=[R])
        are, aim, ldt = sm[:, 0:32], sm[:, 32:64], sm[:, 64:96]
        Ball = alloc(1024)
        Bre, Bim = Ball[:, 0:512], Ball[:, 512:1024]
        Cre, Cim = alloc(512), alloc(512)
        P.dma("sp", Bre, self.d["ssm_B"][jb, 0], ("ss", 1), writes=[R])
        P.dma("sp", Bim, self.d["ssm_B"][jb, 1], ("ss", 1), writes=[R])
        P.dma("sp", Cre, self.d["ssm_C"][jb, 0], ("ss", 1), writes=[R])
        P.dma("sp", Cim, self.d["ssm_C"][jb, 1], ("ss", 1), writes=[R])
        bmask, ident, kvec, nvec, dcol = alloc(128), alloc(128), alloc(256), alloc(32), alloc(64)
        P.dma("sp", bmask, self.d["blockmask"][:, :], ("ss", 2), writes=[R])
        P.dma("sp", ident, self.d["ident128"][:, :], ("ss", 2), writes=[R])
        P.dma("sp", kvec, self.d["kvec"][:, :], ("ss", 2), writes=[R])
        P.dma("sp", nvec, self.d["nvec"][:, :], ("ss", 2), writes=[R])
        P.dma("sp", dcol, self.d["ssm_Dcol"][jb], ("ss", 2), writes=[R])

        dt, lre, th = alloc(32), alloc(32), alloc(32)
        P.act(dt, ldt, AF.Exp, reads=[R], writes=[R])
        tt(lre, are, dt, ALU.mult)
        tt(th, aim, dt, ALU.mult)
        NE_ = 32
        Ere, Eim, t1, t2 = alloc(32 * NE_), alloc(32 * NE_), alloc(32 * NE_), alloc(32 * NE_)
        bc_n = lambda x: x.unsqueeze(2).to_broadcast([128, 32, NE_])
        nv = nvec.unsqueeze(1).to_broadcast([128, 32, NE_])
        tt(v3(t1, 32), bc_n(lre), nv, ALU.mult)
        P.act(t1, t1, AF.Exp, reads=[R], writes=[R])
        tt(v3(t2, 32), bc_n(th), nv, ALU.mult)
        I32 = mybir.dt.int32
        qbuf = alloc(512)

        def sin_of(out, xin, n, add=0.0, rg=None):
            rg = R if rg is None else rg
            if n > 512:
                for h0_ in range(0, n, 512):
                    sin_of(out[:, h0_:h0_ + 512], xin[:, h0_:h0_ + 512], 512, add=add, rg=rg)
                return
            qi = qbuf[:, :n].bitcast(I32)
            x = xin
            if add != 0.0:
                P.ts("dve", out, xin, float(add), None, ALU.add, reads=[rg], writes=[rg])
                x = out
            P.ts("dve", qi, x, float(1.0 / TWO_PI), None, ALU.mult, reads=[rg], writes=[rg])
            P.stt("dve", out, qi, -TWO_PI, x, ALU.mult, ALU.add, reads=[rg], writes=[rg])
            P.act(out, out, AF.Sin, reads=[rg], writes=[rg], scale=0.9999)

        sin_of(Eim, t2, 32 * NE_)
        sin_of(Ere, t2, 32 * NE_, add=PI / 2)
        tt(Ere, Ere, t1, ALU.mult)
        tt(Eim, Eim, t1, ALU.mult)
        E3r, E3i = v3(Ere, 32), v3(Eim, 32)
        P.copy("dve", self.ssmc[:, jb, 0, :], v3(t1, 32)[:, :, 15], reads=[R], writes=[self.ssmc_r])
        P.copy("dve", self.ssmc[:, jb, 1, :], E3r[:, :, 8], reads=[R], writes=[self.ssmc_r])
        P.copy("dve", self.ssmc[:, jb, 2, :], E3i[:, :, 8], reads=[R], writes=[self.ssmc_r])
        P.copy("dve", self.ssmc[:, jb, 3, :], E3r[:, :, 0], reads=[R], writes=[self.ssmc_r])
        P.copy("dve", self.ssmc[:, jb, 4, :], E3i[:, :, 0], reads=[R], writes=[self.ssmc_r])
        nr, den, fre, fim, q1, q2 = alloc(32), alloc(32), alloc(32), alloc(32), alloc(32), alloc(32)
        P.ts("dve", nr, E3r[:, :, 8], -1.0, None, ALU.add, reads=[R], writes=[R])
        ni = E3i[:, :, 8]
        tt(q1, are, are, ALU.mult)
        tt(q2, aim, aim, ALU.mult)
        tt(den, q1, q2, ALU.add)
        P.op("dve", lambda e: e.reciprocal(out=den, in_=den), reads=[R], writes=[R])
        tt(q1, nr, are, ALU.mult)
        tt(q2, ni, aim, ALU.mult)
        tt(fre, q1, q2, ALU.add)
        tt(fre, fre, den, ALU.mult)
        tt(q1, ni, are, ALU.mult)
        tt(q2, nr, aim, ALU.mult)
        tt(fim, q1, q2, ALU.subtract)
        tt(fim, fim, den, ALU.mult)
        bc_c = lambda x: x.unsqueeze(2).to_broadcast([128, 32, 16])
        BBr, BBi = alloc(512), alloc(512)

        def cmul(o_re, o_im, a_re, a_im, b_re, b_im, shape_n, a=None, neg_im=False):
            x1, x2 = t1[:, :shape_n], t2[:, :shape_n]
            if a is not None:
                x1, x2 = v3(x1, a[0]), v3(x2, a[0])
                if len(a) == 3:
                    x1 = x1.rearrange("p a (b c) -> p a b c", b=a[1])
                    x2 = x2.rearrange("p a (b c) -> p a b c", b=a[1])
            tt(x1, a_re, b_re, ALU.mult)
            tt(x2, a_im, b_im, ALU.mult)
            tt(o_re, x1, x2, ALU.subtract)
            tt(x1, a_re, b_im, ALU.mult)
            tt(x2, a_im, b_re, ALU.mult)
            if neg_im:
                P.stt("dve", o_im, x1, -1.0, x2, ALU.mult, ALU.subtract, reads=[R], writes=[R])
            else:
                tt(o_im, x1, x2, ALU.add)

        cmul(v3(BBr, 32), v3(BBi, 32), bc_c(fre), bc_c(fim), v3(Bre, 32), v3(Bim, 32), 512, a=(32,))
        phim = alloc(32)
        ph8 = alloc(32)
        P.ts("dve", ph8, th, 8.0, None, ALU.mult, reads=[R], writes=[R])
        P.ts("dve", qbuf[:, :32].bitcast(I32), ph8, float(1.0 / TWO_PI), None, ALU.mult, reads=[R], writes=[R])
        P.stt("dve", phim, qbuf[:, :32].bitcast(I32), -TWO_PI, ph8, ALU.mult, ALU.add, reads=[R], writes=[R])

        JB = 4
        Xr, Xi, Wr, Wi = alloc(JB * 128), alloc(JB * 128), alloc(JB * 128), alloc(JB * 128)
        Yr, Yi = alloc(JB * 144), alloc(JB * 144)
        targ = alloc(JB * 256)
        tcs = Ball
        tsn = targ
        kst = alloc(JB * 128, pool_s).bitcast(BF16)
        wst = alloc(JB * 128, pool_s).bitcast(BF16)
        cstg = alloc(JB * 128, pool_s).bitcast(BF16)
        ktmp = [alloc(128), alloc(128)]
        x4 = lambda ap, n: ap.rearrange("p (j n c) -> p j n c", j=JB, n=n)
        Rb, Rst, Rt, Rw = Reg(), Reg(), Reg(), Reg()
        for r_ in (Rb, Rst, Rt, Rw):
            r_.w = R.w
            r_.r = list(R.r)
        toks = self.scr_toks[jb]

        def note(tok):
            if toks.get(tok[0], 0) < tok[1]:
                toks[tok[0]] = tok[1]

        pt1, pt2 = alloc(JB * 128), alloc(JB * 128)
        Rp = Reg()
        Rp.w = R.w
        Rp.r = list(R.r)

        def cmul_b(o_re, o_im, a_re, a_im, b_re, b_im, shape_n, a, neg_im=False, eng="dve"):
            ta, tb, rg = (t1, t2, Rb) if eng == "dve" else (pt1, pt2, Rp)
            x1, x2 = v3(ta[:, :shape_n], a[0]), v3(tb[:, :shape_n], a[0])
            x1 = x1.rearrange("p a (b c) -> p a b c", b=a[1])
            x2 = x2.rearrange("p a (b c) -> p a b c", b=a[1])
            kw = dict(reads=[R, rg], writes=[rg])
            kwo = dict(reads=[R, rg], writes=[rg, Rb])
            P.tt(eng, x1, a_re, b_re, ALU.mult, **kw)
            P.tt(eng, x2, a_im, b_im, ALU.mult, **kw)
            P.tt(eng, o_re, x1, x2, ALU.subtract, **kwo)
            P.tt(eng, x1, a_re, b_im, ALU.mult, **kw)
            P.tt(eng, x2, a_im, b_re, ALU.mult, **kw)
            if neg_im:
                P.stt("dve", o_im, x1, -1.0, x2, ALU.mult, ALU.subtract, **kwo)
            else:
                P.tt(eng, o_im, x1, x2, ALU.add, **kwo)

        for j0 in range(0, 32, JB):
            js = slice(j0, j0 + JB)
            tps = slice(j0 // 2, j0 // 2 + JB // 2)
            bb_r = v3(BBr, 32)[:, js, :].unsqueeze(2).to_broadcast([128, JB, 8, 16])
            bb_i = v3(BBi, 32)[:, js, :].unsqueeze(2).to_broadcast([128, JB, 8, 16])
            ex_r = E3r[:, js, 16:24].unsqueeze(3).to_broadcast([128, JB, 8, 16])
            ex_i = E3i[:, js, 16:24].unsqueeze(3).to_broadcast([128, JB, 8, 16])
            cmul_b(x4(Xr, 8), x4(Xi, 8), ex_r, ex_i, bb_r, bb_i, JB * 128, (JB, 8, 16))
            ew_r = E3r[:, js, 24:32].unsqueeze(3).to_broadcast([128, JB, 8, 16])
            ew_i = E3i[:, js, 24:32].unsqueeze(3).to_broadcast([128, JB, 8, 16])
            cmul_b(x4(Wr, 8), x4(Wi, 8), ew_r, ew_i, bb_r, bb_i, JB * 128, (JB, 8, 16))
            ey_r = E3r[:, js, 7:16].unsqueeze(3).to_broadcast([128, JB, 9, 16])
            ey_i = E3i[:, js, 7:16].unsqueeze(3).to_broadcast([128, JB, 9, 16])
            cc_r = v3(Cre, 32)[:, js, :].unsqueeze(2).to_broadcast([128, JB, 9, 16])
            cc_i = v3(Cim, 32)[:, js, :].unsqueeze(2).to_broadcast([128, JB, 9, 16])
            cmul_b(x4(Yr, 9), x4(Yi, 9), ey_r, ey_i, cc_r, cc_i, JB * 144, (JB, 9, 16), neg_im=True)
            c4 = cstg.rearrange("p (j r n) -> p j r n", j=JB, r=2)
            P.copy("dve", c4[:, :, 0, :].rearrange("p j (t c) -> p j t c", t=8), x4(Yr, 9)[:, :, 1:9, :], reads=[Rb], writes=[Rst])
            P.copy("dve", c4[:, :, 1, :].rearrange("p j (t c) -> p j t c", t=8), x4(Yi, 9)[:, :, 1:9, :], reads=[Rb], writes=[Rst])
            note(P.dma("sp", self.d["sc_wkc"][jb][:, tps, 1024:1536], cstg.rearrange("p (a n) -> p a n", a=JB // 2), ("sco", 0), reads=[Rst]))
            for jl in range(JB):
                for g2 in range(2):
                    gi = jl * 2 + g2
                    g = (j0 + jl) * 2 + g2
                    pr = slice(g2 * 64, (g2 + 1) * 64)
                    ps, psr = self.psum()
                    xr_ = x4(Xr, 8)[pr, jl].rearrange("p s c -> p (s c)")
                    xi_ = x4(Xi, 8)[pr, jl].rearrange("p s c -> p (s c)")
                    yr_ = x4(Yr, 9)[pr, jl, 0:8, :].rearrange("p t c -> p (t c)")
                    yi_ = x4(Yi, 9)[pr, jl, 0:8, :].rearrange("p t c -> p (t c)")
                    P.mm(ps[:, 0:128], xr_, yr_, True, False, reads=[Rb], writes=[psr])
                    P.mm(ps[:, 0:128], xi_, yi_, False, True, reads=[Rb], writes=[psr])
                    wr_ = x4(Wr, 8)[pr, jl].rearrange("p s c -> p (s c)")
                    wi_ = x4(Wi, 8)[pr, jl].rearrange("p s c -> p (s c)")
                    ps2, ps2r = self.psum()
                    P.mm(ps2[:, 0:64], wr_, ident[pr, pr], True, True, reads=[Rb, R], writes=[ps2r])
                    P.mm(ps2[:, 64:128], wi_, ident[pr, pr], True, True, reads=[Rb, R], writes=[ps2r])
                    kt = ktmp[gi % 2]
                    P.tt("dve", kt, ps[:, 0:128], bmask, ALU.mult, reads=[psr, R], writes=[Rst])
                    P.stt("dve", kst[:, gi * 128:(gi + 1) * 128], ident, dcol[:, g:g + 1], kt, ALU.mult, ALU.add,
                          reads=[R, Rst], writes=[Rst])
                    P.copy("act", wst[:, gi * 128:(gi + 1) * 128], ps2[:, 0:128], reads=[ps2r], writes=[Rw])
            note(P.dma("sp", self.d["sc_wkc"][jb][:, tps, 512:1024], kst.rearrange("p (a n) -> p a n", a=JB // 2), ("sco", 1), reads=[Rst]))
            note(P.dma("sp", self.d["sc_wkc"][jb][:, tps, 0:512], wst.rearrange("p (a n) -> p a n", a=JB // 2), ("sco", 3), reads=[Rw]))
            P.tt("dve", v3(targ, JB), phim[:, js].unsqueeze(2).to_broadcast([128, JB, 256]), kvec.unsqueeze(1).to_broadcast([128, JB, 256]), ALU.mult,
                 reads=[R, Rt], writes=[Rt])
            sin_of(tcs, targ, JB * 256, add=PI / 2, rg=Rt)
            sin_of(tsn, targ, JB * 256, rg=Rt)
            note(P.dma("sp", self.d["sc_tab"][jb][:, js, 0, :], v3(tcs, JB), ("sco", 2), reads=[Rt]))
            note(P.dma("sp", self.d["sc_tab"][jb][:, js, 1, :], v3(tsn, JB), ("sco", 4), reads=[Rt]))
        for r_ in (Rb, Rst, Rt, Rp, Rw):
            R.r = list(R.r) + ([r_.w] if r_.w else []) + list(r_.r)
        self.arena_regs = [R]
        self.s2_regs = [R]

    def mixer_a(self, l):
        cfg, P = self.cfg, self.P
        j = l // 2
        T, SEQ, NS, KC = cfg.T, cfg.SEQ, cfg.NS, cfg.KC
        nb = len(cfg.blocks)
        self.rmsnorm(self.vcol("mix", l))
        lnb_col = (5 * cfg.DEPTH + 1) * 8 + j * 16
        win = self.d["a_w_in"][j].rearrange("(kc p) f -> p kc f", p=128)
        wout = self.d["a_w_out"][j].rearrange("(kc p) d -> p kc d", p=128)

        s2r = self.carve2(8)
        Cb = self.s2[:, 0:4096].bitcast(F32).rearrange("p (f t) -> p f t", f=16)
        Cbr = s2r[0]
        WsT = self.s2[:, 4096:5120].rearrange("p (h t) -> p h t", h=8)
        WsTr = s2r[1]
        tmpb = [self.s2[:, 5120 + i * 1024:5120 + (i + 1) * 1024].bitcast(F32) for i in range(2)]
        tmpr = s2r[2:4]
        small = self.s2[:, 7168:8192].bitcast(F32)
        smallr = s2r[4]
        mv = small[:, 0:16].rearrange("p (t k) -> p t k", k=2)
        rr = small[:, 16:24]
        bst = small[:, 32:32 + 24]
        w0 = small[:, 64:80]
        Cbs = small[:, 80:96]
        CbS = small[:, 96:96 + 256].rearrange("p (f b) -> p f b", f=16)
        R16 = self.s2[:, 7168 + 2 * 352:7168 + 2 * 352 + 128].rearrange("p (h b) -> p h b", h=8)
        R16r = s2r[5]
        id16 = self.s2[:, 7168 + 2 * 352 + 128:7168 + 2 * 352 + 144]
        v32s = self.sq[0].rearrange("p c t -> p (c t)").bitcast(F32)
        v32r = self.sqr[0]

        P.op("pool", lambda e: e.memset(small, 0.0), writes=[smallr, R16r])
        for hh in range(2):
            t32 = tmpb[hh]
            P.dma("sp", t32.rearrange("p (h t) -> p h t", h=4), self.d["a_wsT"][j][:, hh * 4:(hh + 1) * 4, :], ("mxc", hh), writes=[tmpr[hh]])
        P.dma("sp", small[:, 128:256], self.d["trimask"][:, :], ("mxc", 2), writes=[smallr])
        for hh in range(2):
            P.tt("dve", WsT[:, hh * 4:(hh + 1) * 4, :], tmpb[hh].rearrange("p (h t) -> p h t", h=4),
                 small[:, 128:256].unsqueeze(1).to_broadcast([128, 4, 128]), ALU.mult,
                 reads=[tmpr[hh], smallr], writes=[WsTr])
        P.dma("sp", w0, self.d["a_w0"][j:j + 1, :].partition_broadcast(128), ("mxc", 3), writes=[smallr])
        for hh in range(2):
            ps, psr = self.psum()
            P.mm(ps[:, :512], self.ones[:, :], WsT[:, hh * 4:(hh + 1) * 4, :].rearrange("p h t -> p (h t)"), True, True,
                 reads=[WsTr, self.constr], writes=[psr])
            P.dma("sp", tmpb[hh], self.d["a_bs"][j:j + 1, hh * 512:(hh + 1) * 512].partition_broadcast(128), ("mxc", hh), writes=[tmpr[hh]])
            for hl in range(4):
                h = hh * 4 + hl
                for dcc in range(2):
                    fc = 2 * h + dcc
                    P.stt("dve", Cb[:, fc, :], ps[:, hl * 128:(hl + 1) * 128], self.vecs[:, lnb_col + fc:lnb_col + fc + 1],
                          tmpb[hh][:, hl * 128:(hl + 1) * 128], ALU.mult, ALU.add,
                          reads=[psr, tmpr[hh], self.constr], writes=[Cbr])
        for h in range(8):
            P.ts("dve", Cbs[:, 2 * h:2 * h + 2], self.vecs[:, lnb_col + 2 * h:lnb_col + 2 * h + 2], w0[:, h:h + 1], w0[:, 8 + h:9 + h],
                 ALU.mult, ALU.add, reads=[smallr, self.constr], writes=[smallr])
        P.copy("dve", CbS, Cbs.unsqueeze(2).to_broadcast([128, 16, 16]), reads=[smallr], writes=[smallr])
        P.dma("pool", id16[0:16, :], self.d["ident16"][:, :], ("mxc", 4), writes=[R16r])
        for h in range(8):
            P.ts("dve", R16[0:16, h, :], id16[0:16, :], w0[0:16, h:h + 1], None, ALU.mult, reads=[R16r, smallr], writes=[R16r])

        groups = [[b] for b in range(cfg.NB)]
        groups[-1].append(nb - 1)
        for grp in groups:
            cols = []
            lo = 0
            tiles = []
            for b in grp:
                t0, n = cfg.blocks[b]
                cols.append((b, t0, n, lo))
                if n == 512:
                    for i in range(4):
                        tiles.append((t0 + i * 128, 128, lo + i * 128, False))
                else:
                    tiles.append((t0, n, lo, True))
                lo += n
            NG = lo
            ntl = len(tiles)
            o = 0
            def take(n_):
                nonlocal o
                a = self.arena[:, o:o + n_]
                o += n_
                return a
            uT = take(16 * NG).rearrange("p (f t) -> p f t", f=16)
            vb = [take(2048) for _ in range(ntl)]
            gbc = take(2048)
            Wp = [take(1024).rearrange("p (h t) -> p h t", h=8) for _ in range(2)]
            assert o <= self.ARENA, (o, self.ARENA)
            regs = self.carve(16 + ntl + 1 + 2)
            ur = regs[0:16]
            vr = regs[16:16 + ntl]
            gr = regs[16 + ntl]
            Wpr = regs[16 + ntl + 1:16 + ntl + 3]
            P.dma("pool", gbc, self.d["a_ln_g"][j:j + 1, :].partition_broadcast(128), ("mxg", 0), writes=[gr])

            for p8 in range(8):
                ws, wr = self.wget(win[:, :, 2048 + p8 * 256:2048 + (p8 + 1) * 256], [KC, 256])
                for ti, (tok0, rows, lo_, is_s) in enumerate(tiles):
                    b = nb - 1 if is_s else tok0 // 512
                    ps, psr = self.psum()
                    for c in range(KC):
                        P.mm(ps[:rows, :256], self.hT[:, c, tok0:tok0 + rows], ws[:, c, :], c == 0, c == KC - 1,
                             reads=[wr, self.hr[b]], writes=[psr])
                    if is_s:
                        P.act(v32s[:rows, p8 * 256:(p8 + 1) * 256], ps[:rows, :256], AF.Gelu_apprx_tanh, reads=[psr], writes=[v32r])
                    else:
                        P.act(vb[ti][:rows, p8 * 256:(p8 + 1) * 256], ps[:rows, :256], AF.Gelu_apprx_tanh, reads=[psr], writes=[vr[ti]])
            for ti, (tok0, rows, lo_, is_s) in enumerate(tiles):
                src, srcr = (v32s, v32r) if is_s else (vb[ti], vr[ti])
                for k in range(4):
                    P.op("dve", lambda e, o_=bst[:rows, k * 6:(k + 1) * 6], i_=src[:rows, k * 512:(k + 1) * 512]: e.bn_stats(out=o_, in_=i_),
                         reads=[srcr], writes=[smallr])
                P.op("dve", lambda e, o_=mv[:rows, ti, :], i_=bst[:rows, 0:24]: e.bn_aggr(out=o_, in_=i_), reads=[smallr], writes=[smallr])
            P.act(rr[:, 0:ntl], mv[:, 0:ntl, 1], AF.Sqrt, reads=[smallr, self.constr], writes=[smallr], bias=self.epsc[:, 0:1])
            P.op("dve", lambda e, o_=rr[:, 0:ntl]: e.reciprocal(out=o_, in_=o_), reads=[smallr], writes=[smallr])
            for p8 in range(8):
                ws, wr = self.wget(win[:, :, p8 * 256:(p8 + 1) * 256], [KC, 256])
                for jj in range(2):
                    fc = 2 * p8 + jj
                    for (b, t0, n, lo_) in cols:
                        ps, psr = self.psum()
                        for c in range(KC):
                            P.mm(ps[:, :n], ws[:, c, jj * 128:(jj + 1) * 128], self.hT[:, c, t0:t0 + n], c == 0, c == KC - 1,
                                 reads=[wr, self.hr[b]], writes=[psr])
                        P.act(uT[:, fc, lo_:lo_ + n], ps[:, :n], AF.Gelu_apprx_tanh, reads=[psr], writes=[ur[fc]])
            for ti, (tok0, rows, lo_, is_s) in enumerate(tiles):
                ncols = rows
                if is_s:
                    P.stt("dve", vb[ti][:rows, :], v32s[:rows, :], mv[:rows, ti, 0:1], gbc[:rows, :], ALU.subtract, ALU.mult,
                          reads=[v32r, smallr, gr], writes=[vr[ti]])
                    P.ts("dve", v32s[:rows, :], v32s[:rows, :], mv[:rows, ti, 0:1], rr[:rows, ti:ti + 1], ALU.subtract, ALU.mult,
                         reads=[smallr], writes=[v32r])
                    for k in range(4):
                        gi = self.rot("rs", 2)
                        gt, gtr = self.rs[gi], self.rsr[gi]
                        P.dma("sp", gt[:rows, :], self.d["a_ln_g"][j:j + 1, k * 512:(k + 1) * 512].partition_broadcast(rows), ("mxs", gi), writes=[gtr])
                        P.tt("dve", v32s[:rows, k * 512:(k + 1) * 512], v32s[:rows, k * 512:(k + 1) * 512], gt[:rows, :], ALU.mult,
                             reads=[gtr], writes=[v32r])
                        gi = self.rot("rs", 2)
                        gt, gtr = self.rs[gi], self.rsr[gi]
                        P.dma("sp", gt[:rows, :], self.d["a_ln_b"][j:j + 1, k * 512:(k + 1) * 512].partition_broadcast(rows), ("mxs", gi), writes=[gtr])
                        P.tt("dve", v32s[:rows, k * 512:(k + 1) * 512], v32s[:rows, k * 512:(k + 1) * 512], gt[:rows, :], ALU.add,
                             reads=[gtr], writes=[v32r])
                    self.out_toks.append(P.dma("sp", self.d["chunkv_out"][j], v32s[:rows, :], ("cvo", 0), reads=[v32r]))
                else:
                    P.stt("dve", vb[ti][:rows, :], vb[ti][:rows, :], mv[:rows, ti, 0:1], gbc[:rows, :], ALU.subtract, ALU.mult,
                          reads=[smallr, gr], writes=[vr[ti]])
                wi = self.rot("Wp", 2)
                if is_s:
                    P.ts("dve", Wp[wi][:rows, :, :ncols], R16[:rows, :, :], rr[:rows, ti:ti + 1], None, ALU.mult,
                         reads=[R16r, smallr], writes=[Wpr[wi]])
                else:
                    P.ts("dve", Wp[wi][:, :, :], WsT[:, :, :], rr[:, ti:ti + 1], None, ALU.mult,
                         reads=[WsTr, smallr], writes=[Wpr[wi]])
                for fc4 in range(4):
                    ps, psr = self.psum()
                    for jj in range(4):
                        fc = fc4 * 4 + jj
                        P.mm(ps[:, jj * 128:jj * 128 + ncols], vb[ti][:rows, fc * 128:(fc + 1) * 128], Wp[wi][:rows, fc // 2, :ncols],
                             True, True, reads=[vr[ti], Wpr[wi]], writes=[psr])
                    bi = self.rot("tmpb", 2)
                    tb, tbr = tmpb[bi].rearrange("p (f t) -> p f t", f=4), tmpr[bi]
                    cb = CbS[:, fc4 * 4:(fc4 + 1) * 4, :] if is_s else Cb[:, fc4 * 4:(fc4 + 1) * 4, :]
                    P.tt("dve", tb[:, :, :ncols], ps[:, :].rearrange("p (f t) -> p f t", f=4)[:, :, :ncols], cb, ALU.add,
                         reads=[psr, Cbr, smallr], writes=[tbr])
                    P.tt("dve", uT[:, fc4 * 4:(fc4 + 1) * 4, lo_:lo_ + ncols], tb[:, :, :ncols], uT[:, fc4 * 4:(fc4 + 1) * 4, lo_:lo_ + ncols], ALU.mult,
                         reads=[tbr] + ur[fc4 * 4:(fc4 + 1) * 4], writes=ur[fc4 * 4:(fc4 + 1) * 4])
            for dc in range(KC):
                ws, wr = self.wget(wout[:, :, dc * 128:(dc + 1) * 128], [16, 128])
                for (b, t0, n, lo_) in cols:
                    ps, psr = self.psum()
                    for fc in range(16):
                        P.mm(ps[:, :n], ws[:, fc, :], uT[:, fc, lo_:lo_ + n], fc == 0, fc == 15, reads=[wr, ur[fc]], writes=[psr])
                    P.tt("dve", self.xT[:, dc, t0:t0 + n], ps[:, :n], self.xT[:, dc, t0:t0 + n], ALU.add,
                         reads=[psr, self.xr[b]], writes=[self.xr[b]])

    def mixer_b(self, l):
        cfg, P = self.cfg, self.P
        jb = l // 2
        T, SEQ, NS, KC = cfg.T, cfg.SEQ, cfg.NS, cfg.KC
        NK = SEQ // 8
        nb = len(cfg.blocks)
        allx = list(self.xr)
        allh = list(self.hr)
        self._rmsnorm_blocks(self.vcol("mix", l), None, lambda b: self.hr[b], perm=True)

        U = self.arena[:, 0:64 * NK].rearrange("p (g k) -> p g k", g=64)
        Sel = self.arena[:, 64 * NK:64 * NK + 8192].rearrange("p (m n) -> p m n", m=64)
        assert 64 * NK + 8192 <= self.ARENA
        regs = self.carve(65)
        Ur, Selr = regs[0:64], regs[64]
        s2r = self.carve2(4)
        Us = self.s2[:, 0:1024].rearrange("p (g b) -> p g b", g=64)
        Usr = [Reg() for _ in range(64)]
        for r_ in Usr:
            r_.r = list(s2r[0].r)
        self.s2_regs = list(self.s2_regs) + Usr
        h0 = [self.s2[:, 1024 + i * 1024:2048 + i * 1024].bitcast(F32).rearrange("p (j b) -> p j b", j=32) for i in range(2)]
        Hps = [self.s2[:, 3072 + i * 512:3584 + i * 512].rearrange("p (j b) -> p j b", j=32) for i in range(2)]
        E1 = [self.s2[:, 4096 + i * 1024:5120 + i * 1024].bitcast(F32).rearrange("p (j b) -> p j b", j=32) for i in range(2)]
        stout = self.s2[:, 6144:6272].bitcast(F32).rearrange("p (j r) -> p j r", j=32)
        tmpc = [self.s2[:, 6272 + i * 512:6784 + i * 512].bitcast(F32).rearrange("p (j b) -> p j b", j=32)[:, :, :] if False else
                self.s2[:, 6272 + i * 1024:7296 + i * 1024].bitcast(F32).rearrange("p (j b) -> p j b", j=32) for i in range(1)]
        sreg = s2r[1]
        stout_r = s2r[2]
        for i in range(4):
            P.dma("pool", Sel[:, i * 16:(i + 1) * 16, :], self.d["selmat"][:, i * 2048:(i + 1) * 2048].rearrange("p (m n) -> p m n", m=16),
                  ("sel", 0), writes=[Selr])
        P.dma("sp", h0[0], self.d["ssm_h0"][jb, 0].rearrange("p (j b) -> p j b", j=32), ("h0", 0), writes=[sreg])
        P.dma("sp", h0[1], self.d["ssm_h0"][jb, 1].rearrange("p (j b) -> p j b", j=32), ("h0", 0), writes=[sreg])
        bcb = lambda k_: self.ssmc[:, jb, k_, :].unsqueeze(2).to_broadcast([128, 32, NS])
        t_ = tmpc[0]

        def cm(o_re, o_im, kre, kim):
            P.tt("dve", t_, bcb(kre), h0[0], ALU.mult, reads=[sreg, self.ssmc_r], writes=[sreg])
            P.tt("dve", o_re, bcb(kim), h0[1], ALU.mult, reads=[sreg, self.ssmc_r], writes=[sreg])
            P.tt("dve", o_re, t_, o_re, ALU.subtract, reads=[sreg], writes=[sreg])
            P.tt("dve", t_, bcb(kre), h0[1], ALU.mult, reads=[sreg, self.ssmc_r], writes=[sreg])
            P.tt("dve", o_im, bcb(kim), h0[0], ALU.mult, reads=[sreg, self.ssmc_r], writes=[sreg])
            P.tt("dve", o_im, t_, o_im, ALU.add, reads=[sreg], writes=[sreg])

        cm(E1[0], E1[1], 1, 2)
        hp32 = [self.rs[0][:, :].rearrange("p (j b) -> p j b", j=32), self.rs[1][:, :].rearrange("p (j b) -> p j b", j=32)]
        for i in range(2):
            P.wait_tok("dve", self.rsr[i].w)
            for t in self.rsr[i].r:
                P.wait_tok("dve", t)
        cm(hp32[0], hp32[1], 3, 4)
        P.copy("dve", Hps[0], hp32[0], reads=[sreg], writes=[sreg])
        P.copy("dve", Hps[1], hp32[1], reads=[sreg], writes=[sreg])
        for i in range(2):
            self.rsr[i].w = sreg.w
            self.rsr[i].r = []

        for g in range(64):
            c, gl = g // 8, g % 8
            ps, psr = self.psum()
            for s_ in range(8):
                P.mm(ps[:, :NK], Sel[:, gl * 8 + s_, :], self.hT[:, c, s_ * NK:(s_ + 1) * NK], s_ == 0, s_ == 7,
                     reads=[Selr] + allh, writes=[psr])
            P.mm(ps[:, NK:NK + NS], Sel[:, gl * 8 + 7, :], self.hT[:, c, SEQ:SEQ + NS], True, True, reads=[Selr] + allh, writes=[psr])
            eng = "act" if g % 2 == 0 else "dve"
            P.copy(eng, U[:, g, :], ps[:, :NK], reads=[psr], writes=[Ur[g]])
            P.copy(eng, Us[:, g, :], ps[:, NK:NK + NS], reads=[psr], writes=[Usr[g]])

        HW = self.hT.rearrange("p c t -> p (c t)")
        o = 0
        def wtake(n_):
            nonlocal o
            a = HW[:, o:o + n_]
            o += n_
            return a
        N2 = 2 * NK
        f32v = lambda ap: ap.bitcast(F32).rearrange("p (i k) -> p i k", i=2)
        tabs = [[f32v(wtake(2 * N2)) for _ in range(2)] for _ in range(2)]
        zr, zi, m1, m2 = (f32v(wtake(2 * N2)) for _ in range(4))
        Hp = [[wtake(2 * (NK + NS)).rearrange("p (i k) -> p i k", i=2) for _ in range(2)] for _ in range(2)]
        assert o <= KC * T, (o, KC * T)
        inh = {}
        for r_ in allh:
            for t in ([r_.w] if r_.w else []) + list(r_.r):
                if t is not None and inh.get(t[0], 0) < t[1]:
                    inh[t[0]] = t[1]
        def wreg():
            r_ = Reg()
            r_.r = list(inh.items())
            return r_
        tabr = [wreg() for _ in range(2)]
        zreg = wreg()
        Hpr = [wreg() for _ in range(2)]
        for b_ in range(2):
            for ri in range(2):
                P.op("pool", lambda e, a_=Hp[b_][ri]: e.memset(a_, 0.0), writes=[Hpr[b_]])
        rho = self.ssmc[:, jb, 0, :]
        for k_, v_ in self.scr_toks[jb].items():
            P.wait_tok("sp", (k_, v_))
        st = {}

        def stage_dh(tp):
            wk, wkr = self.wget(self.d["sc_wkc"][jb][:, tp, :], [1536], dep=jb)
            W_ = wk[:, 0:512].rearrange("p (g n) -> p g n", g=4)
            Km = wk[:, 512:1024].rearrange("p (g n) -> p g n", g=4)
            Cc = wk[:, 1024:1536].rearrange("p (i r n) -> p i r n", i=2, r=2)
            bi = tp % 2
            cs, sn = tabs[bi]
            P.dma("sp", cs, self.d["sc_tab"][jb][:, 2 * tp:2 * tp + 2, 0, 0:NK], ("tab", bi), writes=[tabr[bi]])
            P.dma("sp", sn, self.d["sc_tab"][jb][:, 2 * tp:2 * tp + 2, 1, 0:NK], ("tab", bi), writes=[tabr[bi]])
            pre, prer = self.psum()
            pim, pimr = self.psum()
            pss, pssr = self.psum()
            for i in range(2):
                for g2 in range(2):
                    gi = 2 * i + g2
                    g = 4 * tp + gi
                    pr = slice(g2 * 64, (g2 + 1) * 64)
                    P.mm(pre[pr, i * NK:(i + 1) * NK], W_[:, gi, 0:64], U[:, g, :], True, True, reads=[wkr, Ur[g]], writes=[prer])
                    P.mm(pim[pr, i * NK:(i + 1) * NK], W_[:, gi, 64:128], U[:, g, :], True, True, reads=[wkr, Ur[g]], writes=[pimr])
                    P.mm(pss[pr, i * NS:(i + 1) * NS], W_[:, gi, 0:64], Us[:, g, :], True, True, reads=[wkr, Usr[g]], writes=[pssr])
                    P.mm(pss[pr, 32 + i * NS:32 + (i + 1) * NS], W_[:, gi, 64:128], Us[:, g, :], True, True, reads=[wkr, Usr[g]], writes=[pssr])
            st[tp] = (W_, Km, Cc, wkr, bi, cs, sn, pre, prer, pim, pimr, pss, pssr)

        def stage_dve(tp):
            (W_, Km, Cc, wkr, bi, cs, sn, pre, prer, pim, pimr, pss, pssr) = st[tp]
            dre = pre[:, 0:N2].rearrange("p (i k) -> p i k", i=2)
            dim_ = pim[:, 0:N2].rearrange("p (i k) -> p i k", i=2)
            P.tt("dve", m1, dre, cs, ALU.mult, reads=[prer, tabr[bi]], writes=[zreg])
            P.tt("dve", m2, dim_, sn, ALU.mult, reads=[pimr, tabr[bi]], writes=[zreg])
            P.tt("dve", zr, m1, m2, ALU.add, reads=[zreg], writes=[zreg])
            P.tt("dve", m1, dim_, cs, ALU.mult, reads=[pimr, tabr[bi]], writes=[zreg])
            P.tt("dve", m2, dre, sn, ALU.mult, reads=[prer, tabr[bi]], writes=[zreg])
            P.tt("dve", zi, m1, m2, ALU.subtract, reads=[zreg], writes=[zreg])
            for i in range(2):
                j = 2 * tp + i
                rb = rho[:, j:j + 1].to_broadcast([128, NK])
                P.op("dve", lambda e, o_=m1[:, i, :], d0=rb, d1=zr[:, i, :]: e.tensor_tensor_scan(out=o_, data0=d0, data1=d1, initial=0.0, op0=ALU.mult, op1=ALU.add),
                     reads=[zreg, self.ssmc_r], writes=[zreg])
                P.op("dve", lambda e, o_=m2[:, i, :], d0=rb, d1=zi[:, i, :]: e.tensor_tensor_scan(out=o_, data0=d0, data1=d1, initial=0.0, op0=ALU.mult, op1=ALU.add),
                     reads=[zreg, self.ssmc_r], writes=[zreg])
            P.tt("dve", zr, m1, cs, ALU.mult, reads=[zreg, tabr[bi]], writes=[zreg])
            P.tt("dve", zi, m2, sn, ALU.mult, reads=[zreg, tabr[bi]], writes=[zreg])
            P.tt("dve", zr, zr, zi, ALU.subtract, reads=[zreg], writes=[zreg])
            P.tt("dve", zi, m2, cs, ALU.mult, reads=[zreg, tabr[bi]], writes=[zreg])
            P.tt("dve", m1, m1, sn, ALU.mult, reads=[zreg, tabr[bi]], writes=[zreg])
            P.tt("dve", zi, zi, m1, ALU.add, reads=[zreg], writes=[zreg])
            hb = tp % 2
            Hre, Him = Hp[hb]
            P.copy("act", Hre[:, :, 1:NK], zr[:, :, 0:NK - 1], reads=[zreg], writes=[Hpr[hb]])
            P.copy("act", Him[:, :, 1:NK], zi[:, :, 0:NK - 1], reads=[zreg], writes=[Hpr[hb]])
            P.copy("act", Hre[:, :, NK:NK + NS], Hps[0][:, 2 * tp:2 * tp + 2, :], reads=[sreg], writes=[Hpr[hb]])
            P.copy("act", Him[:, :, NK:NK + NS], Hps[1][:, 2 * tp:2 * tp + 2, :], reads=[sreg], writes=[Hpr[hb]])
            P.copy("act", stout[:, 2 * tp:2 * tp + 2, 0], zr[:, :, NK - 1], reads=[zreg], writes=[stout_r])
            P.copy("act", stout[:, 2 * tp:2 * tp + 2, 1], zi[:, :, NK - 1], reads=[zreg], writes=[stout_r])
            P.tt("dve", E1[0][:, 2 * tp:2 * tp + 2, :], E1[0][:, 2 * tp:2 * tp + 2, :], pss[:, 0:32].rearrange("p (i b) -> p i b", i=2), ALU.add,
                 reads=[pssr, sreg], writes=[sreg])
            P.tt("dve", E1[1][:, 2 * tp:2 * tp + 2, :], E1[1][:, 2 * tp:2 * tp + 2, :], pss[:, 32:64].rearrange("p (i b) -> p i b", i=2), ALU.add,
                 reads=[pssr, sreg], writes=[sreg])

        def stage_y(tp):
            (W_, Km, Cc, wkr, bi, cs, sn, pre, prer, pim, pimr, pss, pssr) = st[tp]
            hb = tp % 2
            Hre, Him = Hp[hb]
            for gi in range(4):
                i, g2 = gi // 2, gi % 2
                g = 4 * tp + gi
                pr = slice(g2 * 64, (g2 + 1) * 64)
                ps, psr = self.psum()
                for (c0, c1, rhsU, rr_) in ((0, NK, U[:, g, :], Ur[g]), (NK, NK + NS, Us[:, g, :], Usr[g])):
                    P.mm(ps[:, c0:c1], Km[:, gi, :], rhsU, True, False, reads=[wkr, rr_], writes=[psr])
                    P.mm(ps[:, c0:c1], Cc[pr, i, 0, :], Hre[pr, i, c0:c1], False, False, reads=[wkr, Hpr[hb]], writes=[psr])
                    P.mm(ps[:, c0:c1], Cc[pr, i, 1, :], Him[pr, i, c0:c1], False, True, reads=[wkr, Hpr[hb]], writes=[psr])
                P.act(U[:, g, :], ps[:, 0:NK], AF.Gelu_apprx_tanh, reads=[psr], writes=[Ur[g]])
                P.act(Us[:, g, :], ps[:, NK:NK + NS], AF.Gelu_apprx_tanh, reads=[psr], writes=[Usr[g]])

        stage_dh(0)
        for tp in range(16):
            stage_dve(tp)
            if tp + 1 < 16:
                stage_dh(tp + 1)
            stage_y(tp)
        self.out_toks.append(P.dma("sp", self.d["ssm_pr_out"][jb], stout, ("sso", 0), reads=[stout_r]))
        self.out_toks.append(P.dma("sp", self.d["ssm_s_out"][jb, 0].rearrange("p (j b) -> p j b", j=32), E1[0], ("sso", 0), reads=[sreg]))
        self.out_toks.append(P.dma("sp", self.d["ssm_s_out"][jb, 1].rearrange("p (j b) -> p j b", j=32), E1[1], ("sso", 0), reads=[sreg]))

        inh2 = {}
        for r_ in tabr + [zreg] + Hpr:
            for t in ([r_.w] if r_.w else []) + list(r_.r):
                if t is not None and inh2.get(t[0], 0) < t[1]:
                    inh2[t[0]] = t[1]
        for r_ in allh:
            r_.r = list(r_.r) + list(inh2.items())
        for c in range(KC):
            for s_ in range(8):
                ps, psr = self.psum()
                for gl in range(8):
                    P.mm(ps[:, :NK], Sel[:, s_ * 8 + gl, :], U[:, 8 * c + gl, :], gl == 0, gl == 7, reads=[Selr, Ur[8 * c + gl]], writes=[psr])
                eng = "act" if s_ % 2 == 0 else "dve"
                P.copy(eng, self.hT[:, c, s_ * NK:(s_ + 1) * NK], ps[:, :NK], reads=[psr], writes=allh)
            ps, psr = self.psum()
            for gl in range(8):
                P.mm(ps[:, :NS], Sel[:, 7 * 8 + gl, :], Us[:, 8 * c + gl, :], gl == 0, gl == 7, reads=[Selr, Usr[8 * c + gl]], writes=[psr])
            P.copy("dve", self.hT[:, c, SEQ:SEQ + NS], ps[:, :NS], reads=[psr], writes=allh)

        wglu = self.d["b_w_glu"][jb].rearrange("(kc p) n -> p kc n", p=128)
        sgr = self.carve(3)
        sgb = [self.arena[:, i * 512:(i + 1) * 512] for i in range(3)]
        pblocks = [(q0, 512) for q0 in range(0, SEQ, 512)] + [(SEQ, NS)]
        for d2 in range(KC // 2):
            wv_, wvr = self.wget(wglu[:, :, d2 * 256:(d2 + 1) * 256], [KC, 256])
            wg_, wgr = self.wget(wglu[:, :, 1024 + d2 * 256:1024 + (d2 + 1) * 256], [KC, 256])
            for jj in range(2):
                dc = d2 * 2 + jj
                for (q0, n) in pblocks:
                    pv, pvr = self.psum()
                    pg, pgr = self.psum()
                    for c in range(KC):
                        P.mm(pv[:, :n], wv_[:, c, jj * 128:(jj + 1) * 128], self.hT[:, c, q0:q0 + n], c == 0, c == KC - 1, reads=[wvr] + allh, writes=[pvr])
                    for c in range(KC):
                        P.mm(pg[:, :n], wg_[:, c, jj * 128:(jj + 1) * 128], self.hT[:, c, q0:q0 + n], c == 0, c == KC - 1, reads=[wgr] + allh, writes=[pgr])
                    si = self.rot("sgb", 3)
                    sg32 = sgb[si].bitcast(F32)
                    P.act(sgb[si][:, :n], pg[:, :n], AF.Sigmoid, reads=[pgr], writes=[sgr[si]])
                    if n == 512:
                        ns_ = 512 // NK
                        s0 = q0 // NK
                        xv = self.xT[:, dc, 0:SEQ].rearrange("p (k s) -> p s k", s=8)[:, s0:s0 + ns_, :]
                        pvv = pv[:, :n].rearrange("p (s k) -> p s k", s=ns_)
                        sgv = sgb[si][:, :n].rearrange("p (s k) -> p s k", s=ns_)
                    else:
                        xv = self.xT[:, dc, q0:q0 + n]
                        pvv = pv[:, :n]
                        sgv = sgb[si][:, :n]
                    P.tt("dve", sgv, pvv, sgv, ALU.mult, reads=[pvr, sgr[si]], writes=[sgr[si]])
                    P.tt("dve", xv, xv, sgv, ALU.add, reads=[sgr[si]] + allx, writes=allx)

    def xattn(self, l):
        cfg, P = self.cfg, self.P
        T, SEQ, NS, NM, KC = cfg.T, cfg.SEQ, cfg.NS, cfg.NMEM, cfg.KC
        nb = len(cfg.blocks)
        o = 0
        def take(n):
            nonlocal o
            a = self.arena[:, o:o + n]
            o += n
            return a
        qT = take(KC * T).rearrange("p (c t) -> p c t", c=KC)
        Eb = [take(1024).rearrange("p (m t) -> p m t", m=2) for _ in range(2)]
        kT = take(KC * NM).rearrange("p (c m) -> p c m", c=KC)
        V = take(2 * 1024).rearrange("p (m f) -> p m f", m=2)
        mnT = take(KC * NM).rearrange("p (c m) -> p c m", c=KC)
        Es = self.s2[:, 7424:7552]
        assert o <= self.ARENA, (o, self.ARENA)
        regs = self.carve(nb + 2 + 1 + 1 + 1 + 1)
        qr = regs[0:nb]
        Er = regs[nb:nb + 2]
        kTr, Vr, mnr, Esr = regs[nb + 2], regs[nb + 3], regs[nb + 4], regs[nb + 5]
        s2r = self.carve2(1 + 2 + 2 + 1 + 1)
        Esr = s2r[6]
        memT = self.s2[:, 0:4096].bitcast(F32).rearrange("p (c m) -> p c m", c=KC)
        memr = s2r[0]
        rd = [self.s2[:, 4096 + i * 1024:4096 + (i + 1) * 1024].bitcast(F32) for i in range(2)]
        rdr = s2r[1:3]
        ost = [self.s2[:, 6144 + i * 512:6144 + (i + 1) * 512].bitcast(F32) for i in range(2)]
        ostr = s2r[3:5]
        rds = self.s2[:, 7168:7168 + 128].bitcast(F32)
        rdsr = s2r[5]
        gm = self.vcol("mem", l)

        P.dma("sp", memT, self.d["memT"].rearrange("(c p) m -> p c m", p=128), ("mem", 0), writes=[memr])
        for c in range(KC):
            P.stt("dve", mnT[:, c, :], memT[:, c, :], self.vecs[:, gm + c:gm + c + 1], self.mrstd[:, :],
                  ALU.mult, ALU.mult, reads=[memr, self.mrstd_r, self.constr], writes=[mnr])
        wk = self.d["x_wk"][l].rearrange("(kc p) d -> p kc d", p=128)
        wv = self.d["x_wv"][l].rearrange("(kc p) d -> p kc d", p=128)
        wq = self.d["x_wq"][l].rearrange("(kc p) d -> p kc d", p=128)
        wo = self.d["x_wo"][l].rearrange("(kc p) d -> p kc d", p=128)
        kout = self.d["memk_out"][l].rearrange("(c p) m -> p c m", p=128)
        vout = self.d["memv_out"][l].rearrange("(mc p) f -> p mc f", p=128)
        for d2 in range(KC // 2):
            ws, wr = self.wget(wk[:, :, d2 * 256:(d2 + 1) * 256], [KC, 256])
            for j in range(2):
                dc = d2 * 2 + j
                ps, psr = self.psum()
                for c in range(KC):
                    P.mm(ps[:, :NM], ws[:, c, j * 128:(j + 1) * 128], mnT[:, c, :], c == 0, c == KC - 1,
                         reads=[wr, mnr], writes=[psr])
                i = self.rot("ost", 2)
                P.copy("dve", ost[i][:, :NM], ps[:, :NM], reads=[psr], writes=[ostr[i]])
                P.copy("act", kT[:, dc, :], ost[i][:, :NM], reads=[ostr[i]], writes=[kTr])
                self.out_toks.append(P.dma("sp", kout[:, dc, :], ost[i][:, :NM], ("ost", i), reads=[ostr[i]]))
        for n4 in range(4):
            ws, wr = self.wget(wv[:, :, n4 * 256:(n4 + 1) * 256], [KC, 256])
            for mc in range(2):
                ps, psr = self.psum()
                for c in range(KC):
                    P.mm(ps[:, :256], mnT[:, c, mc * 128:(mc + 1) * 128], ws[:, c, :], c == 0, c == KC - 1,
                         reads=[wr, mnr], writes=[psr])
                i = self.rot("ost", 2)
                P.copy("dve", ost[i][:, :256], ps[:, :256], reads=[psr], writes=[ostr[i]])
                P.copy("act", V[:, mc, n4 * 256:(n4 + 1) * 256], ost[i][:, :256], reads=[ostr[i]], writes=[Vr])
                self.out_toks.append(P.dma("sp", vout[:, mc, n4 * 256:(n4 + 1) * 256], ost[i][:, :256], ("ost", i), reads=[ostr[i]]))

        self.rmsnorm(self.vcol("x", l))
        for d2 in range(KC // 2):
            ws, wr = self.wget(wq[:, :, d2 * 256:(d2 + 1) * 256], [KC, 256])
            for j in range(2):
                dc = d2 * 2 + j
                for b, (t0, n) in enumerate(cfg.blocks):
                    ps, psr = self.psum()
                    for c in range(KC):
                        P.mm(ps[:, :n], ws[:, c, j * 128:(j + 1) * 128], self.hT[:, c, t0:t0 + n], c == 0, c == KC - 1,
                             reads=[wr, self.hr[b]], writes=[psr])
                    P.op("act", lambda e, o_=qT[:, dc, t0:t0 + n], i_=ps[:, :n]: e.mul(out=o_, in_=i_, mul=0.0625),
                         reads=[psr], writes=[qr[b]])

        NE = 4
        Esb = [Es[:, i * 8:(i + 1) * 8] for i in range(NE)]
        Esr_ = [Reg() for _ in range(NE)]
        for r_ in Esr_:
            r_.r = list(Esr.r)
        rdsb = [rds[:, i * 4:(i + 1) * 4] for i in range(NE)]
        rdsr_ = [Reg() for _ in range(NE)]
        for r_ in rdsr_:
            r_.r = list(rdsr.r)
        self.s2_regs = list(self.s2_regs) + Esr_ + rdsr_
        ckT = self.d["cacheKT"][l]
        cV = self.d["cacheV"][l]

        def sample_S(s):
            ks, kr = self.wget(ckT[s].rearrange("(c p) m -> p c m", p=128), [KC, NM])
            ps, psr = self.psum()
            for mc in range(2):
                for h in range(4):
                    col = mc * 4 + h
                    for dcc in range(2):
                        P.mm(ps[:, col:col + 1], ks[:, 2 * h + dcc, mc * 128:(mc + 1) * 128],
                             qT[:, 2 * h + dcc, SEQ + s:SEQ + s + 1], dcc == 0, dcc == 1,
                             reads=[kr, qr[nb - 1]], writes=[psr])
            ei = s % NE
            P.act(Esb[ei], ps[:, 0:8], AF.Exp, reads=[psr], writes=[Esr_[ei]])

        def sample_PV(s):
            ei = s % NE
            vs, vr = self.wget(cV[s].rearrange("(mc p) f -> p mc f", p=128), [2, 1024])
            ps, psr = self.psum()
            for hdc in range(KC):
                h = hdc // 2
                for mc in range(2):
                    P.mm(ps[:, hdc:hdc + 1], vs[:, mc, hdc * 128:(hdc + 1) * 128],
                         Esb[ei][:, mc * 4 + h:mc * 4 + h + 1], mc == 0, mc == 1, reads=[vr, Esr_[ei]], writes=[psr])
            for mc in range(2):
                P.mm(ps[:, 8:12], self.ones[:, :], Esb[ei][:, mc * 4:(mc + 1) * 4], mc == 0, mc == 1,
                     reads=[Esr_[ei], self.constr], writes=[psr])
            P.op("dve", lambda e, o_=rdsb[ei], i_=ps[:, 8:12]: e.reciprocal(out=o_, in_=i_), reads=[psr], writes=[rdsr_[ei]])
            P.tt("dve", self.hT[:, :, SEQ + s].rearrange("p (h d) -> p h d", h=4),
                 ps[:, 0:8].rearrange("p (h d) -> p h d", h=4),
                 rdsb[ei].unsqueeze(2).to_broadcast([128, 4, 2]), ALU.mult,
                 reads=[psr, rdsr_[ei]], writes=[self.hr[nb - 1]])

        def prompt_block(b):
            t0, n = cfg.blocks[b]
            for h in range(4):
                ei = self.rot("E", 2)
                E, Er_ = Eb[ei], Er[ei]
                for mc in range(2):
                    ps, psr = self.psum()
                    for dcc in range(2):
                        P.mm(ps[:, :n], kT[:, 2 * h + dcc, mc * 128:(mc + 1) * 128], qT[:, 2 * h + dcc, t0:t0 + n],
                             dcc == 0, dcc == 1, reads=[kTr, qr[b]], writes=[psr])
                    P.act(E[:, mc, :n], ps[:, :n], AF.Exp, reads=[psr], writes=[Er_])
                ps, psr = self.psum()
                for mc in range(2):
                    P.mm(ps[:, :n], self.ones[:, :], E[:, mc, :n], mc == 0, mc == 1, reads=[Er_, self.constr], writes=[psr])
                ri = self.rot("rd", 2)
                P.op("dve", lambda e, o_=rd[ri][:, :n], i_=ps[:, :n]: e.reciprocal(out=o_, in_=i_), reads=[psr], writes=[rdr[ri]])
                for dcc in range(2):
                    ps, psr = self.psum()
                    for mc in range(2):
                        P.mm(ps[:, :n], V[:, mc, (2 * h + dcc) * 128:(2 * h + dcc + 1) * 128], E[:, mc, :n],
                             mc == 0, mc == 1, reads=[Vr, Er_], writes=[psr])
                    P.tt("dve", self.hT[:, 2 * h + dcc, t0:t0 + n], ps[:, :n], rd[ri][:, :n], ALU.mult,
                         reads=[psr, rdr[ri]], writes=[self.hr[b]])

        order = []
        per = (NS + cfg.NB - 1) // max(cfg.NB, 1)
        sdone = 0
        for b in range(cfg.NB):
            order.append(("p", b))
            for s in range(sdone, min(NS, sdone + per)):
                order.append(("s", s))
            sdone = min(NS, sdone + per)
        for s in range(sdone, NS):
            order.append(("s", s))
        prev_s = None
        for kind, i in order:
            if kind == "p":
                prompt_block(i)
            else:
                sample_S(i)
                if prev_s is not None:
                    sample_PV(prev_s)
                prev_s = i
        sample_PV(prev_s)
        for d2 in range(KC // 2):
            ws, wr = self.wget(wo[:, :, d2 * 256:(d2 + 1) * 256], [KC, 256])
            for j in range(2):
                dc = d2 * 2 + j
                for b, (t0, n) in enumerate(cfg.blocks):
                    ps, psr = self.psum()
                    for c in range(KC):
                        P.mm(ps[:, :n], ws[:, c, j * 128:(j + 1) * 128], self.hT[:, c, t0:t0 + n], c == 0, c == KC - 1,
                             reads=[wr, self.hr[b]], writes=[psr])
                    P.tt("dve", self.xT[:, dc, t0:t0 + n], ps[:, :n], self.xT[:, dc, t0:t0 + n], ALU.add,
                         reads=[psr, self.xr[b]], writes=[self.xr[b]])

    def epilogue(self):
        cfg, P = self.cfg, self.P
        gcol = self.vcol("final", 0)
        yT_d = self.d["yT"].rearrange("(c p) t -> p c t", p=128)
        nb = len(cfg.blocks)
        assert cfg.KC * 512 * 2 * 2 <= self.ARENA
        yr_all = self.carve(2)
        stages = [self.arena[:, i * cfg.KC * 1024:(i + 1) * cfg.KC * 1024].bitcast(F32).rearrange("p (c t) -> p c t", c=cfg.KC)
                  for i in range(2)]
        toks = []
        out_view = {}
        class _V:
            def __init__(s2, b):
                s2.b = b
        self._rmsnorm_blocks(gcol, lambda b, c, t0, n: stages[b % 2][:, c, 0:n], lambda b: yr_all[b % 2],
                             after=lambda b, t0, n: toks.append(
                                 P.dma("sp", yT_d[:, :, t0:t0 + n], stages[b % 2][:, :, 0:n], ("y", b % 2), reads=[yr_all[b % 2]])))
        return toks

    def build_body(self):
        cfg = self.cfg
        self.prologue()
        stages = getattr(cfg, "stages", ("ffn1", "mixer", "xattn", "ffn2"))
        if "mixer" in stages or "ssmsetup" in stages:
            for jb in range(self.NBL):
                self.ssm_setup(jb)
        for l in range(cfg.DEPTH):
            if "ffn1" in stages:
                self.ffn("ffn1", l, self.vcol("ffn1", l))
            if "mixer" in stages:
                if l % 2 == 0:
                    self.mixer_a(l)
                elif hasattr(self, "mixer_b"):
                    self.mixer_b(l)
            if "xattn" in stages:
                self.xattn(l)
            if "ffn2" in stages:
                self.ffn("ffn2", l, self.vcol("ffn2", l))
        return self.epilogue()


def build_program(cfg):
    from contextlib import ExitStack
    pieces = None
    for pass_i in range(2):
        nc = bass.Bass("TRN2", target_bir_lowering=False)
        with ExitStack() as stack:
            B = Builder(cfg, nc, dry_pieces=pieces)
            B.declare(stack)
            out_toks = B.build_body()
            if pass_i == 0:
                pieces = B.pieces
                continue
            P = B.P
            final = {}
            for t in list(out_toks) + list(B.out_toks):
                if t is not None and final.get(t[0], 0) < t[1]:
                    final[t[0]] = t[1]
            for k_, v_ in final.items():
                P.wait_tok("sp", (k_, v_))
            sems = {}
            for k in P.semkeys:
                sems[k] = stack.enter_context(nc.semaphore("s_" + "_".join(str(x) for x in (k if not isinstance(k[1], tuple) else (k[0],) + k[1]))))
            P.emit(nc, sems)
        return nc, B


def pack_vecs(cfg, inputs):
    L = cfg.DEPTH
    cols = []
    for nm in ("norm_ffn1", "norm_mix", "norm_x", "norm_ffn2", "norm_mem"):
        for l in range(L):
            cols.append(np.asarray(inputs[nm][l], np.float32).reshape(8, 128).T)
    cols.append(np.asarray(inputs["norm_final"], np.float32).reshape(8, 128).T)
    for j in range((L + 1) // 2):
        cols.append(np.asarray(inputs["a_ln_b"][j], np.float32).reshape(16, 128).T)
    return np.ascontiguousarray(np.concatenate(cols, axis=1))


def make_in_maps(cfg, inputs, n_cores):
    ns = cfg.NS
    vecs = pack_vecs(cfg, inputs)
    shared = {"vecs": vecs}
    for nm in ("x_wq", "x_wk", "x_wv", "x_wo"):
        shared[nm] = np.ascontiguousarray(inputs[nm][:cfg.DEPTH], dtype=np.float32)
    for nm in ("ffn1", "ffn2"):
        for s in ("_wg", "_wu", "_wd"):
            shared[nm + s] = np.ascontiguousarray(inputs[nm + s][:cfg.DEPTH], dtype=np.float32)
    NA = (cfg.DEPTH + 1) // 2
    shared["a_w_in"] = np.ascontiguousarray(inputs["a_w_in"][:NA], dtype=np.float32)
    shared["a_w_out"] = np.ascontiguousarray(inputs["a_w_out"][:NA], dtype=np.float32)
    shared["a_ln_g"] = np.ascontiguousarray(inputs["a_ln_g"][:NA], dtype=np.float32)
    shared["a_ln_b"] = np.ascontiguousarray(inputs["a_ln_b"][:NA], dtype=np.float32)
    ws = np.asarray(inputs["a_w_s"][:NA], np.float32)
    shared["a_wsT"] = np.ascontiguousarray(ws.transpose(0, 3, 1, 2))
    shared["a_bs"] = np.ascontiguousarray(np.asarray(inputs["a_b_s"][:NA], np.float32).reshape(NA, 1024))
    shared["a_w0"] = np.ascontiguousarray(np.concatenate([ws[:, :, 0, 0], np.asarray(inputs["a_b_s"][:NA], np.float32)[:, :, 0]], axis=1))
    NBL = max(cfg.DEPTH // 2, 1)
    f32 = lambda a: np.asarray(a, np.float32)
    gp = lambda a: f32(a)[:NBL].reshape(NBL, 32, 2, 64).transpose(0, 2, 3, 1).reshape(NBL, 128, 32)
    ldt = np.repeat(f32(inputs["b_log_dt"])[:NBL, :, None], 64, axis=2)
    shared["ssm_small"] = np.ascontiguousarray(np.stack([gp(inputs["b_a_re"]), gp(inputs["b_a_im"]), gp(ldt)], axis=2))
    gpc = lambda a: f32(a)[:NBL].reshape(NBL, 32, 2, 64, 16).transpose(0, 2, 3, 1, 4).reshape(NBL, 128, 512)
    shared["ssm_B"] = np.ascontiguousarray(np.stack([gpc(inputs["b_b_re"]), gpc(inputs["b_b_im"])], axis=1))
    gcp = lambda a: gpc(f32(a)[:NBL].transpose(0, 1, 3, 2))
    shared["ssm_C"] = np.ascontiguousarray(np.stack([gcp(inputs["b_c_re"]), gcp(inputs["b_c_im"])], axis=1))
    dcol = np.tile(f32(inputs["b_d"])[:NBL].transpose(0, 2, 1)[:, None, :, :], (1, 8, 1, 1)).reshape(NBL, 128, 64)
    shared["ssm_Dcol"] = np.ascontiguousarray(dcol)
    shared["b_w_glu"] = np.ascontiguousarray(f32(inputs["b_w_glu"])[:NBL])
    sidx = np.arange(128) // 16
    shared["blockmask"] = (sidx[None, :] >= sidx[:, None]).astype(np.float32)
    shared["ident128"] = np.eye(128, dtype=np.float32)
    shared["kvec"] = np.tile(np.arange(256, dtype=np.float32)[None, :], (128, 1))
    nvals = np.concatenate([np.arange(-7, 9), -np.arange(0, 8), np.arange(7, -1, -1)]).astype(np.float32)
    shared["nvec"] = np.tile(nvals[None, :], (128, 1))
    sel = np.zeros((128, 8, 8, 128), np.float32)
    eye = np.eye(128, dtype=np.float32)
    for a_ in range(8):
        for b_ in range(8):
            sel[:, a_, b_, 16 * b_:16 * b_ + 16] = eye[:, 16 * a_:16 * a_ + 16]
    shared["selmat"] = np.ascontiguousarray(sel.reshape(128, 64 * 128))
    shared["trimask"] = np.triu(np.ones((128, 128), np.float32))
    shared["ident16"] = np.eye(16, dtype=np.float32)
    maps = []
    for c in range(n_cores):
        xp = np.asarray(inputs["x_prompt"][c, :cfg.SEQ], np.float32)
        xs = np.asarray(inputs["x_sample"][c * ns:(c + 1) * ns, 0], np.float32)
        xT = np.ascontiguousarray(np.concatenate([xp, xs], axis=0).T)
        m = dict(shared)
        m["xT"] = xT
        m["memT"] = np.ascontiguousarray(np.asarray(inputs["mem_prompt"][c], np.float32).T)
        ck = np.asarray(inputs["cache_mem_k"][:cfg.DEPTH, c * ns:(c + 1) * ns], np.float32).reshape(cfg.DEPTH, ns, cfg.NMEM, cfg.D)
        m["cacheKT"] = np.ascontiguousarray(ck.transpose(0, 1, 3, 2))
        m["cacheV"] = np.ascontiguousarray(np.asarray(inputs["cache_mem_v"][:cfg.DEPTH, c * ns:(c + 1) * ns], np.float32).reshape(cfg.DEPTH, ns, cfg.NMEM, cfg.D))
        h0 = lambda a: f32(a)[:NBL, c * ns:(c + 1) * ns].reshape(NBL, ns, 32, 2, 64).transpose(0, 3, 4, 2, 1).reshape(NBL, 128, 32 * ns)
        m["ssm_h0"] = np.ascontiguousarray(np.stack([h0(inputs["state_ssm_re"]), h0(inputs["state_ssm_im"])], axis=1))
        maps.append(m)
    return maps


def kernel(**inputs):
    cfg = Cfg()
    n = 8
    nc, B = build_program(cfg)
    in_maps = make_in_maps(cfg, inputs, n)
    res = run_bass_kernel_spmd(nc, in_maps, core_ids=list(range(n)))
    R_ = res.results
    L, NS, SEQ = cfg.DEPTH, cfg.NS, cfg.SEQ
    f32 = np.float32
    ys = [r["yT"] for r in R_]
    y_prompt = np.stack([y[:, :SEQ].T for y in ys]).astype(f32)
    y_sample = np.concatenate([y[:, SEQ:].T for y in ys])[:, None, :].astype(f32)
    mem_k = np.stack([r["memk_out"].transpose(0, 2, 1).reshape(L, 256, 4, 256) for r in R_], axis=1).astype(f32)
    mem_v = np.stack([r["memv_out"].reshape(L, 256, 4, 256) for r in R_], axis=1).astype(f32)
    NBL = L // 2
    gp = lambda a: a.reshape(2, 64, 32).transpose(2, 0, 1).reshape(64, 64)
    gs = lambda a: a.reshape(2, 64, 32, NS).transpose(3, 2, 0, 1).reshape(NS, 64, 64)
    ssm_re_p = np.stack([np.stack([gp(r["ssm_pr_out"][j, :, :, 0]) for r in R_]) for j in range(NBL)]).astype(f32)
    ssm_im_p = np.stack([np.stack([gp(r["ssm_pr_out"][j, :, :, 1]) for r in R_]) for j in range(NBL)]).astype(f32)
    ssm_re_s = np.stack([np.concatenate([gs(r["ssm_s_out"][j, 0]) for r in R_]) for j in range(NBL)]).astype(f32)
    ssm_im_s = np.stack([np.concatenate([gs(r["ssm_s_out"][j, 1]) for r in R_]) for j in range(NBL)]).astype(f32)
    chunk_v = np.concatenate([r["chunkv_out"] for r in R_], axis=1)[:, :, None, :].astype(f32)
    return (y_prompt, y_sample, mem_k, mem_v, ssm_re_p, ssm_im_p, ssm_re_s, ssm_im_s, chunk_v)
```

```python
import numpy as np
import concourse.bass as bass
import concourse.mybir as mybir
from concourse.bass_utils import run_bass_kernel_spmd

F32 = mybir.dt.float32
BF16 = mybir.dt.bfloat16
AF = mybir.ActivationFunctionType
ALU = mybir.AluOpType
EPS = 1e-6


class Reg:
    __slots__ = ("w", "r", "excl")

    def __init__(self, excl=False):
        self.w = None
        self.r = []
        self.excl = excl


class Prog:
    ENGS = ("pe", "act", "dve", "pool", "sp")
    EPOCH = 30000

    def __init__(self):
        self.streams = {e: [] for e in self.ENGS}
        self.cnt = {e: 0 for e in self.ENGS}
        self.epoch = {e: 0 for e in self.ENGS}
        self.waited = {e: {} for e in self.ENGS}
        self.dma_cnt = {}
        self.semkeys = []

    def _ekey(self, eng):
        k = (eng, self.epoch[eng])
        if k not in self.semkeys:
            self.semkeys.append(k)
        return k

    def _wait(self, eng, tok):
        if tok is None:
            return
        key, val = tok
        if key[0] == eng and eng == "pe":
            return
        if self.waited[eng].get(key, 0) >= val:
            return
        self.waited[eng][key] = val
        self.streams[eng].append(("wait", key, val))

    def op(self, eng, fn, reads=(), writes=(), inc=True):
        for r in reads:
            self._wait(eng, r.w)
            if r.excl:
                for t in r.r:
                    if t is not None and t[0][0] != eng:
                        self._wait(eng, t)
        for w in writes:
            self._wait(eng, w.w)
            for t in w.r:
                self._wait(eng, t)
        key = self._ekey(eng)
        if inc:
            self.cnt[eng] += 1
            tok = (key, self.cnt[eng])
            self.streams[eng].append(("op", fn, key))
            if self.cnt[eng] >= self.EPOCH:
                self.epoch[eng] += 1
                self.cnt[eng] = 0
        else:
            tok = (key, self.cnt[eng] + 1)
            self.streams[eng].append(("op", fn, None))
        for r in reads:
            r.r.append(tok)
        for w in writes:
            w.w = tok
            w.r = []
        return tok

    def dma(self, q, out, in_, semkey, reads=(), writes=()):
        for r in reads:
            self._wait(q, r.w)
        for w in writes:
            self._wait(q, w.w)
            for t in w.r:
                self._wait(q, t)
        key = ("dma", semkey)
        if key not in self.semkeys:
            self.semkeys.append(key)
        self.dma_cnt[key] = self.dma_cnt.get(key, 0) + 16
        tok = (key, self.dma_cnt[key])
        self.streams[q].append(("dma", out, in_, key))
        for r in reads:
            r.r.append(tok)
        for w in writes:
            w.w = tok
            w.r = []
        return tok

    def wait_tok(self, eng, tok):
        self._wait(eng, tok)

    def mm(self, out, lhsT, rhs, start, stop, reads=(), writes=()):
        return self.op("pe", lambda e: e.matmul(out, lhsT, rhs, start=start, stop=stop),
                       reads=reads, writes=writes, inc=stop)

    def act(self, out, in_, func, reads=(), writes=(), eng="act", **kw):
        return self.op(eng, lambda e: e.activation(out=out, in_=in_, func=func, **kw),
                       reads=reads, writes=writes)

    def ts(self, eng, out, in0, s1, s2, op0, op1=None, reads=(), writes=()):
        if op1 is None:
            return self.op(eng, lambda e: e.tensor_scalar(out=out, in0=in0, scalar1=s1, scalar2=None, op0=op0),
                           reads=reads, writes=writes)
        return self.op(eng, lambda e: e.tensor_scalar(out=out, in0=in0, scalar1=s1, scalar2=s2, op0=op0, op1=op1),
                       reads=reads, writes=writes)

    def tt(self, eng, out, in0, in1, op, reads=(), writes=()):
        return self.op(eng, lambda e: e.tensor_tensor(out=out, in0=in0, in1=in1, op=op),
                       reads=reads, writes=writes)

    def stt(self, eng, out, in0, scalar, in1, op0, op1, reads=(), writes=()):
        return self.op(eng, lambda e: e.scalar_tensor_tensor(out=out, in0=in0, scalar=scalar, in1=in1, op0=op0, op1=op1),
                       reads=reads, writes=writes)

    def copy(self, eng, out, in_, reads=(), writes=()):
        if eng == "act":
            return self.op(eng, lambda e: e.copy(out=out, in_=in_), reads=reads, writes=writes)
        return self.op(eng, lambda e: e.tensor_copy(out=out, in_=in_), reads=reads, writes=writes)

    def emit(self, nc, sems):
        engmap = {"pe": "tensor", "act": "scalar", "dve": "vector", "pool": "gpsimd", "sp": "sync"}

        def run(e, stream):
            for item in stream:
                if item[0] == "wait":
                    e.wait_ge(sems[item[1]], item[2])
                elif item[0] == "op":
                    ins = item[1](e)
                    if item[2] is not None:
                        ins.then_inc(sems[item[2]], 1)
                else:
                    e.dma_start(out=item[1], in_=item[2]).then_inc(sems[item[3]], 16)

        with nc.Block() as block:
            for eng in self.ENGS:
                getattr(block, engmap[eng])(lambda e, s=self.streams[eng]: run(e, s))


class Cfg:
    def __init__(self, seq=2048, ns=16, depth=4, dff=2816):
        self.D = 1024
        self.KC = 8
        self.SEQ = seq
        self.NS = ns
        self.T = seq + ns
        self.DEPTH = depth
        self.DFF = dff
        self.FC = dff // 128
        self.NB = seq // 512
        self.blocks = [(i * 512, 512) for i in range(self.NB)] + [(seq, ns)]
        self.NMEM = 256


WSLOT = 2048
NW = 5


class Builder:
    def __init__(self, cfg, nc, dry_pieces=None):
        self.cfg = cfg
        self.nc = nc
        self.P = Prog()
        self.dry = dry_pieces is None
        self.pieces = [] if self.dry else dry_pieces
        self.piece_i = 0
        self.issued = 0
        self.ps_i = 0
        self.tmp_i = {}

    def psum(self):
        i = self.ps_i % 8
        self.ps_i += 1
        return self.ps[i], self.psr[i]

    def wget(self, dram_ap, shape, dep=None):
        k = self.piece_i
        self.piece_i += 1
        if self.dry:
            self.pieces.append((dram_ap, shape, dep))
        else:
            last = min(k + NW - 2, len(self.pieces) - 1)
            while self.issued <= last:
                j = self.issued
                d_ap, shp, dep_ = self.pieces[j]
                if dep_ is not None:
                    for k_, v_ in self.scr_toks[dep_].items():
                        self.P.wait_tok("pool", (k_, v_))
                self.P.dma("pool", self._slot_ap(j % NW, shp), d_ap, ("w", j % NW), writes=[self.wreg[j % NW]])
                self.issued += 1
        return self._slot_ap(k % NW, shape), self.wreg[k % NW]

    def _slot_ap(self, s, shape):
        n = int(np.prod(shape))
        assert n <= WSLOT, (shape, n)
        ap = self.wslot[s][:, 0:n]
        if len(shape) == 2:
            return ap.rearrange("p (a b) -> p a b", a=shape[0])
        if len(shape) == 3:
            return ap.rearrange("p (a b c) -> p a b c", a=shape[0], b=shape[1])
        return ap

    def carve(self, n):
        best = {}
        for r in getattr(self, "arena_regs", []):
            for t in ([r.w] if r.w else []) + list(r.r):
                if t is not None and best.get(t[0], 0) < t[1]:
                    best[t[0]] = t[1]
        inherit = [(k, v) for k, v in best.items()]
        regs = []
        for _ in range(n):
            r = Reg()
            r.r = list(inherit)
            regs.append(r)
        self.arena_regs = regs
        return regs

    def carve2(self, n):
        best = {}
        for r in getattr(self, "s2_regs", []):
            for t in ([r.w] if r.w else []) + list(r.r):
                if t is not None and best.get(t[0], 0) < t[1]:
                    best[t[0]] = t[1]
        inherit = [(k, v) for k, v in best.items()]
        regs = []
        for _ in range(n):
            r = Reg()
            r.r = list(inherit)
            regs.append(r)
        self.s2_regs = regs
        return regs

    def rot(self, name, n):
        i = self.tmp_i.get(name, 0)
        self.tmp_i[name] = i + 1
        return i % n

    def declare(self, stack):
        cfg, nc = self.cfg, self.nc
        L, D, T = cfg.DEPTH, cfg.D, cfg.T
        din = lambda name, shape: nc.dram_tensor(name, list(shape), F32, kind="ExternalInput").ap()
        dout = lambda name, shape: nc.dram_tensor(name, list(shape), F32, kind="ExternalOutput").ap()
        self.d = {}
        self.d["xT"] = din("xT", [D, T])
        self.NA = (L + 1) // 2
        self.NBL = L // 2
        self.NV = (5 * L + 1) * 8 + self.NA * 16
        self.d["vecs"] = din("vecs", [128, self.NV])
        for nm in ("ffn1", "ffn2"):
            self.d[nm + "_wg"] = din(nm + "_wg", [L, D, cfg.DFF])
            self.d[nm + "_wu"] = din(nm + "_wu", [L, D, cfg.DFF])
            self.d[nm + "_wd"] = din(nm + "_wd", [L, cfg.DFF, D])
        self.d["memT"] = din("memT", [D, cfg.NMEM])
        for nm in ("x_wq", "x_wk", "x_wv", "x_wo"):
            self.d[nm] = din(nm, [L, D, D])
        self.d["cacheKT"] = din("cacheKT", [L, cfg.NS, D, cfg.NMEM])
        self.d["cacheV"] = din("cacheV", [L, cfg.NS, cfg.NMEM, D])
        NA = self.NA
        self.d["a_w_in"] = din("a_w_in", [NA, D, 4096])
        self.d["a_w_out"] = din("a_w_out", [NA, 2048, D])
        self.d["a_ln_g"] = din("a_ln_g", [NA, 2048])
        self.d["a_ln_b"] = din("a_ln_b", [NA, 2048])
        self.d["a_wsT"] = din("a_wsT", [NA, 128, 8, 128])
        self.d["a_bs"] = din("a_bs", [NA, 1024])
        self.d["a_w0"] = din("a_w0", [NA, 16])
        self.d["trimask"] = din("trimask", [128, 128])
        self.d["ident16"] = din("ident16", [16, 16])
        self.d["chunkv_out"] = dout("chunkv_out", [NA, cfg.NS, 2048])
        NBL = max(self.NBL, 1)
        self.d["ssm_small"] = din("ssm_small", [NBL, 128, 3, 32])
        self.d["ssm_B"] = din("ssm_B", [NBL, 2, 128, 512])
        self.d["ssm_C"] = din("ssm_C", [NBL, 2, 128, 512])
        self.d["ssm_Dcol"] = din("ssm_Dcol", [NBL, 128, 64])
        self.d["b_w_glu"] = din("b_w_glu", [NBL, D, 2 * D])
        self.d["ssm_h0"] = din("ssm_h0", [NBL, 2, 128, 512])
        self.d["blockmask"] = din("blockmask", [128, 128])
        self.d["ident128"] = din("ident128", [128, 128])
        self.d["kvec"] = din("kvec", [128, 256])
        self.d["nvec"] = din("nvec", [128, 32])
        self.d["selmat"] = din("selmat", [128, 64 * 128])
        bdout = lambda name, shape: nc.dram_tensor(name, list(shape), BF16, kind="ExternalOutput").ap()
        self.d["sc_wkc"] = bdout("sc_wkc", [NBL, 128, 16, 1536])
        self.d["sc_tab"] = dout("sc_tab", [NBL, 128, 32, 2, 256])
        self.d["ssm_pr_out"] = dout("ssm_pr_out", [NBL, 128, 32, 2])
        self.d["ssm_s_out"] = dout("ssm_s_out", [NBL, 2, 128, 512])
        self.d["yT"] = dout("yT", [D, T])
        self.d["memk_out"] = dout("memk_out", [L, D, cfg.NMEM])
        self.d["memv_out"] = dout("memv_out", [L, cfg.NMEM, D])
        self.out_toks = []

        sb = lambda name, shape, dt: stack.enter_context(nc.sbuf_tensor(name, list(shape), dt))
        self.xT = sb("xT_sb", [128, cfg.KC, T], F32)
        self.hT = sb("hT_sb", [128, cfg.KC, T], BF16)
        self.ARENA = max(12 * T, 24768)
        self.arena = sb("arena", [128, self.ARENA], BF16)
        self.wslot = [sb(f"wslot{i}", [128, WSLOT], BF16) for i in range(NW)]
        self.wreg = [Reg() for _ in range(NW)]
        self.vecs = sb("vecs_sb", [128, self.NV], F32)
        self.mrstd = sb("mrstd_sb", [128, cfg.NMEM], F32)
        self.mrstd_r = Reg()
        self.ssmc = sb("ssmc_sb", [128, max(self.NBL, 1), 5, 32], F32)
        self.ssmc_r = Reg()
        self.scr_toks = [dict() for _ in range(max(self.NBL, 1))]
        self.epsc = sb("epsc_sb", [128, 1], F32)
        self.ones = sb("ones_sb", [128, 128], BF16)
        self.sq = [sb(f"sq{i}", [128, cfg.KC, 512], BF16) for i in range(2)]
        self.sqr = [Reg() for _ in range(2)]
        self.rs = [sb(f"rs{i}", [128, 512], F32) for i in range(2)]
        self.rsr = [Reg() for _ in range(2)]
        self.S2N = 8192
        self.s2 = sb("s2", [128, self.S2N], BF16)
        self.ps = [stack.enter_context(nc.psum_tensor(f"ps{i}", [128, 512], F32)) for i in range(8)]
        self.psr = [Reg(excl=True) for _ in range(8)]
        nb = len(cfg.blocks)
        self.xr = [Reg() for _ in range(nb)]
        self.hr = [Reg() for _ in range(nb)]
        self.constr = Reg()

    def vcol(self, kind, layer):
        L = self.cfg.DEPTH
        order = {"ffn1": 0, "mix": 1, "x": 2, "ffn2": 3, "mem": 4}
        if kind == "final":
            return 5 * L * 8
        return (order[kind] * L + layer) * 8

    def prologue(self):
        cfg, P = self.cfg, self.P
        xT_d = self.d["xT"].rearrange("(c p) t -> p c t", p=128)
        for b, (t0, n) in enumerate(cfg.blocks):
            P.dma("sp", self.xT[:, :, t0:t0 + n], xT_d[:, :, t0:t0 + n], ("x", b), writes=[self.xr[b]])
        P.dma("sp", self.vecs[:, :], self.d["vecs"][:, :], ("c", 0), writes=[self.constr])
        P.op("pool", lambda e: e.memset(self.ones[:, :], 1.0), writes=[self.constr])
        P.op("pool", lambda e: e.memset(self.epsc[:, :], EPS), writes=[self.constr])
        (mr,) = self.carve2(1)
        memT = self.s2[:, 0:4096].bitcast(F32).rearrange("p (c m) -> p c m", c=cfg.KC)
        P.dma("sp", memT, self.d["memT"].rearrange("(c p) m -> p c m", p=128), ("c", 1), writes=[mr])
        sq, sqr = self.sq[0], self.sqr[0]
        P.act(sq[:, :, :cfg.NMEM], memT, AF.Square, reads=[mr], writes=[sqr])
        ps, psr = self.psum()
        for c in range(cfg.KC):
            P.mm(ps[:, :cfg.NMEM], self.ones[:, :], sq[:, c, :cfg.NMEM], c == 0, c == cfg.KC - 1,
                 reads=[sqr, self.constr], writes=[psr])
        P.act(self.mrstd[:, :], ps[:, :cfg.NMEM], AF.Sqrt, reads=[psr, self.constr], writes=[self.mrstd_r],
              scale=1.0 / cfg.D, bias=self.epsc[:, 0:1])
        P.op("dve", lambda e: e.reciprocal(out=self.mrstd[:, :], in_=self.mrstd[:, :]), reads=[self.mrstd_r], writes=[self.mrstd_r])

    def rmsnorm(self, gcol, out=None, out_regs=None):
        out = self.hT if out is None else out
        out_regs = self.hr if out_regs is None else out_regs
        self._rmsnorm_blocks(gcol, lambda b, c, t0, n: out[:, c, t0:t0 + n], lambda b: out_regs[b])

    def _rmsnorm_blocks(self, gcol, out_ap, out_reg, after=None, perm=False):
        cfg, P = self.cfg, self.P
        blocks = cfg.blocks
        st = {}

        def stage_a(b):
            t0, n = blocks[b]
            i = self.rot("sq", 2)
            sq, sqr = self.sq[i], self.sqr[i]
            P.act(sq[:, :, :n], self.xT[:, :, t0:t0 + n], AF.Square, reads=[self.xr[b]], writes=[sqr])
            ps, psr = self.psum()
            for c in range(cfg.KC):
                P.mm(ps[:, :n], self.ones[:, :], sq[:, c, :n], c == 0, c == cfg.KC - 1,
                     reads=[sqr, self.constr], writes=[psr])
            st[b] = (ps, psr)

        def stage_b(b):
            t0, n = blocks[b]
            ps, psr = st[b]
            i = self.rot("rs", 2)
            rs, rsr = self.rs[i], self.rsr[i]
            P.act(rs[:, :n], ps[:, :n], AF.Sqrt, reads=[psr, self.constr], writes=[rsr],
                  scale=1.0 / cfg.D, bias=self.epsc[:, 0:1])
            P.op("dve", lambda e: e.reciprocal(out=rs[:, :n], in_=rs[:, :n]), reads=[rsr], writes=[rsr])
            for c in range(cfg.KC):
                if perm and n == 512:
                    o_ = self.hT[:, c, 0:cfg.SEQ].rearrange("p (s k) -> p s k", s=8)[:, :, t0 // 8:(t0 + n) // 8]
                    i0 = self.xT[:, c, t0:t0 + n].rearrange("p (k s) -> p s k", s=8)
                    i1 = rs[:, :n].rearrange("p (k s) -> p s k", s=8)
                    P.stt("dve", o_, i0, self.vecs[:, gcol + c:gcol + c + 1], i1, ALU.mult, ALU.mult,
                          reads=[self.xr[b], rsr, self.constr], writes=list(self.hr))
                    continue
                oap = self.hT[:, c, t0:t0 + n] if out_ap is None else out_ap(b, c, t0, n)
                P.stt("dve", oap, self.xT[:, c, t0:t0 + n], self.vecs[:, gcol + c:gcol + c + 1],
                      rs[:, :n], ALU.mult, ALU.mult, reads=[self.xr[b], rsr, self.constr], writes=[out_reg(b)])
            if after is not None:
                after(b, t0, n)

        nb = len(blocks)
        stage_a(0)
        for b in range(nb):
            if b + 1 < nb:
                stage_a(b + 1)
            stage_b(b)

    def ffn(self, nm, layer, gcol):
        cfg, P = self.cfg, self.P
        self.rmsnorm(gcol)
        wg = self.d[nm + "_wg"][layer].rearrange("(kc p) f -> p kc f", p=128)
        wu = self.d[nm + "_wu"][layer].rearrange("(kc p) f -> p kc f", p=128)
        wd = self.d[nm + "_wd"][layer].rearrange("(kc p) d -> p kc d", p=128)
        npieces = cfg.DFF // 256
        halves = [list(range(0, (npieces + 1) // 2)), list(range((npieces + 1) // 2, npieces))]
        nb = len(cfg.blocks)
        sgr = self.carve2(3)
        sg = [self.s2[:, i * 512:(i + 1) * 512] for i in range(3)]
        for pcs in halves:
            nfc = 2 * len(pcs)
            assert nfc * cfg.T <= self.ARENA
            act = self.arena[:, 0:nfc * cfg.T].rearrange("p (f t) -> p f t", f=nfc)
            flat = self.carve(nfc * nb)
            actr = [[flat[f * nb + b] for b in range(nb)] for f in range(nfc)]
            for pi, pc in enumerate(pcs):
                wgs, wgr = self.wget(wg[:, :, pc * 256:(pc + 1) * 256], [cfg.KC, 256])
                wus, wur = self.wget(wu[:, :, pc * 256:(pc + 1) * 256], [cfg.KC, 256])
                for j in range(2):
                    fl = pi * 2 + j
                    for b, (t0, n) in enumerate(cfg.blocks):
                        pg, pgr = self.psum()
                        pu, pur = self.psum()
                        for c in range(cfg.KC):
                            P.mm(pg[:, :n], wgs[:, c, j * 128:(j + 1) * 128], self.hT[:, c, t0:t0 + n],
                                 c == 0, c == cfg.KC - 1, reads=[wgr, self.hr[b]], writes=[pgr])
                        for c in range(cfg.KC):
                            P.mm(pu[:, :n], wus[:, c, j * 128:(j + 1) * 128], self.hT[:, c, t0:t0 + n],
                                 c == 0, c == cfg.KC - 1, reads=[wur, self.hr[b]], writes=[pur])
                        si = self.rot("sg", 3)
                        P.act(sg[si][:, :n], pg[:, :n], AF.Silu, reads=[pgr], writes=[sgr[si]])
                        P.tt("dve", act[:, fl, t0:t0 + n], sg[si][:, :n], pu[:, :n], ALU.mult,
                             reads=[sgr[si], pur], writes=[actr[fl][b]])
            k0 = pcs[0] * 2
            for dc in range(cfg.KC):
                wds, wdr = self.wget(wd[:, k0:k0 + nfc, dc * 128:(dc + 1) * 128], [nfc, 128])
                for b, (t0, n) in enumerate(cfg.blocks):
                    po, por = self.psum()
                    for fl in range(nfc):
                        P.mm(po[:, :n], wds[:, fl, :], act[:, fl, t0:t0 + n],
                             fl == 0, fl == nfc - 1, reads=[wdr, actr[fl][b]], writes=[por])
                    P.stt("dve", self.xT[:, dc, t0:t0 + n], po[:, :n], 0.5, self.xT[:, dc, t0:t0 + n],
                          ALU.mult, ALU.add, reads=[por, self.xr[b]], writes=[self.xr[b]])


    def ssm_setup(self, jb):
        cfg, P = self.cfg, self.P
        PI = float(np.pi)
        TWO_PI = float(2 * np.pi)
        A32 = self.arena[:, 0:(self.ARENA // 2) * 2].bitcast(F32)
        S32 = self.s2[:, :].bitcast(F32)
        pool_a = [A32, 0, self.ARENA // 2]
        pool_s = [S32, 0, self.S2N // 2]
        (ar,) = self.carve(1)
        (sr,) = self.carve2(1)
        R = Reg()
        R.r = list(ar.r) + list(sr.r)

        def alloc(n, pool=None):
            for pl in ([pool] if pool else [pool_a, pool_s]):
                if pl[1] + n <= pl[2]:
                    a = pl[0][:, pl[1]:pl[1] + n]
                    pl[1] += n
                    return a
            raise AssertionError(f"setup scratch overflow need {n} a={pool_a[1]}/{pool_a[2]} s={pool_s[1]}/{pool_s[2]}")

        def v3(ap, a):
            return ap.rearrange("p (a b) -> p a b", a=a)

        def tt(out, a, b, op, eng="dve"):
            P.tt(eng, out, a, b, op, reads=[R], writes=[R])

        sm = alloc(96)
        P.dma("sp", v3(sm, 3), self.d["ssm_small"][jb], ("ss", 0), writes=[R])
        are, aim, ldt = sm[:, 0:32], sm[:, 32:64], sm[:, 64:96]
        Ball = alloc(1024)
        Bre, Bim = Ball[:, 0:512], Ball[:, 512:1024]
        Cre, Cim = alloc(512), alloc(512)
        P.dma("sp", Bre, self.d["ssm_B"][jb, 0], ("ss", 1), writes=[R])
        P.dma("sp", Bim, self.d["ssm_B"][jb, 1], ("ss", 1), writes=[R])
        P.dma("sp", Cre, self.d["ssm_C"][jb, 0], ("ss", 1), writes=[R])
        P.dma("sp", Cim, self.d["ssm_C"][jb, 1], ("ss", 1), writes=[R])
        bmask, ident, kvec, nvec, dcol = alloc(128), alloc(128), alloc(256), alloc(32), alloc(64)
        P.dma("sp", bmask, self.d["blockmask"][:, :], ("ss", 2), writes=[R])
        P.dma("sp", ident, self.d["ident128"][:, :], ("ss", 2), writes=[R])
        P.dma("sp", kvec, self.d["kvec"][:, :], ("ss", 2), writes=[R])
        P.dma("sp", nvec, self.d["nvec"][:, :], ("ss", 2), writes=[R])
        P.dma("sp", dcol, self.d["ssm_Dcol"][jb], ("ss", 2), writes=[R])

        dt, lre, th = alloc(32), alloc(32), alloc(32)
        P.act(dt, ldt, AF.Exp, reads=[R], writes=[R])
        tt(lre, are, dt, ALU.mult)
        tt(th, aim, dt, ALU.mult)
        NE_ = 32
        Ere, Eim, t1, t2 = alloc(32 * NE_), alloc(32 * NE_), alloc(32 * NE_), alloc(32 * NE_)
        bc_n = lambda x: x.unsqueeze(2).to_broadcast([128, 32, NE_])
        nv = nvec.unsqueeze(1).to_broadcast([128, 32, NE_])
        tt(v3(t1, 32), bc_n(lre), nv, ALU.mult)
        P.act(t1, t1, AF.Exp, reads=[R], writes=[R])
        tt(v3(t2, 32), bc_n(th), nv, ALU.mult)
        I32 = mybir.dt.int32
        qbuf = alloc(512)

        def sin_of(out, xin, n, add=0.0, rg=None):
            rg = R if rg is None else rg
            if n > 512:
                for h0_ in range(0, n, 512):
                    sin_of(out[:, h0_:h0_ + 512], xin[:, h0_:h0_ + 512], 512, add=add, rg=rg)
                return
            qi = qbuf[:, :n].bitcast(I32)
            x = xin
            if add != 0.0:
                P.ts("dve", out, xin, float(add), None, ALU.add, reads=[rg], writes=[rg])
                x = out
            P.ts("dve", qi, x, float(1.0 / TWO_PI), None, ALU.mult, reads=[rg], writes=[rg])
            P.stt("dve", out, qi, -TWO_PI, x, ALU.mult, ALU.add, reads=[rg], writes=[rg])
            P.act(out, out, AF.Sin, reads=[rg], writes=[rg], scale=0.9999)

        sin_of(Eim, t2, 32 * NE_)
        sin_of(Ere, t2, 32 * NE_, add=PI / 2)
        tt(Ere, Ere, t1, ALU.mult)
        tt(Eim, Eim, t1, ALU.mult)
        E3r, E3i = v3(Ere, 32), v3(Eim, 32)
        P.copy("dve", self.ssmc[:, jb, 0, :], v3(t1, 32)[:, :, 15], reads=[R], writes=[self.ssmc_r])
        P.copy("dve", self.ssmc[:, jb, 1, :], E3r[:, :, 8], reads=[R], writes=[self.ssmc_r])
        P.copy("dve", self.ssmc[:, jb, 2, :], E3i[:, :, 8], reads=[R], writes=[self.ssmc_r])
        P.copy("dve", self.ssmc[:, jb, 3, :], E3r[:, :, 0], reads=[R], writes=[self.ssmc_r])
        P.copy("dve", self.ssmc[:, jb, 4, :], E3i[:, :, 0], reads=[R], writes=[self.ssmc_r])
        nr, den, fre, fim, q1, q2 = alloc(32), alloc(32), alloc(32), alloc(32), alloc(32), alloc(32)
        P.ts("dve", nr, E3r[:, :, 8], -1.0, None, ALU.add, reads=[R], writes=[R])
        ni = E3i[:, :, 8]
        tt(q1, are, are, ALU.mult)
        tt(q2, aim, aim, ALU.mult)
        tt(den, q1, q2, ALU.add)
        P.op("dve", lambda e: e.reciprocal(out=den, in_=den), reads=[R], writes=[R])
        tt(q1, nr, are, ALU.mult)
        tt(q2, ni, aim, ALU.mult)
        tt(fre, q1, q2, ALU.add)
        tt(fre, fre, den, ALU.mult)
        tt(q1, ni, are, ALU.mult)
        tt(q2, nr, aim, ALU.mult)
        tt(fim, q1, q2, ALU.subtract)
        tt(fim, fim, den, ALU.mult)
        bc_c = lambda x: x.unsqueeze(2).to_broadcast([128, 32, 16])
        BBr, BBi = alloc(512), alloc(512)

        def cmul(o_re, o_im, a_re, a_im, b_re, b_im, shape_n, a=None, neg_im=False):
            x1, x2 = t1[:, :shape_n], t2[:, :shape_n]
            if a is not None:
                x1, x2 = v3(x1, a[0]), v3(x2, a[0])
                if len(a) == 3:
                    x1 = x1.rearrange("p a (b c) -> p a b c", b=a[1])
                    x2 = x2.rearrange("p a (b c) -> p a b c", b=a[1])
            tt(x1, a_re, b_re, ALU.mult)
            tt(x2, a_im, b_im, ALU.mult)
            tt(o_re, x1, x2, ALU.subtract)
            tt(x1, a_re, b_im, ALU.mult)
            tt(x2, a_im, b_re, ALU.mult)
            if neg_im:
                P.stt("dve", o_im, x1, -1.0, x2, ALU.mult, ALU.subtract, reads=[R], writes=[R])
            else:
                tt(o_im, x1, x2, ALU.add)

        cmul(v3(BBr, 32), v3(BBi, 32), bc_c(fre), bc_c(fim), v3(Bre, 32), v3(Bim, 32), 512, a=(32,))
        phim = alloc(32)
        ph8 = alloc(32)
        P.ts("dve", ph8, th, 8.0, None, ALU.mult, reads=[R], writes=[R])
        P.ts("dve", qbuf[:, :32].bitcast(I32), ph8, float(1.0 / TWO_PI), None, ALU.mult, reads=[R], writes=[R])
        P.stt("dve", phim, qbuf[:, :32].bitcast(I32), -TWO_PI, ph8, ALU.mult, ALU.add, reads=[R], writes=[R])

        JB = 4
        Xr, Xi, Wr, Wi = alloc(JB * 128), alloc(JB * 128), alloc(JB * 128), alloc(JB * 128)
        Yr, Yi = alloc(JB * 144), alloc(JB * 144)
        targ = alloc(JB * 256)
        tcs = Ball
        tsn = targ
        kst = alloc(JB * 128, pool_s).bitcast(BF16)
        wst = alloc(JB * 128, pool_s).bitcast(BF16)
        cstg = alloc(JB * 128, pool_s).bitcast(BF16)
        ktmp = [alloc(128), alloc(128)]
        x4 = lambda ap, n: ap.rearrange("p (j n c) -> p j n c", j=JB, n=n)
        Rb, Rst, Rt, Rw = Reg(), Reg(), Reg(), Reg()
        for r_ in (Rb, Rst, Rt, Rw):
            r_.w = R.w
            r_.r = list(R.r)
        toks = self.scr_toks[jb]

        def note(tok):
            if toks.get(tok[0], 0) < tok[1]:
                toks[tok[0]] = tok[1]

        pt1, pt2 = alloc(JB * 128), alloc(JB * 128)
        Rp = Reg()
        Rp.w = R.w
        Rp.r = list(R.r)

        def cmul_b(o_re, o_im, a_re, a_im, b_re, b_im, shape_n, a, neg_im=False, eng="dve"):
            ta, tb, rg = (t1, t2, Rb) if eng == "dve" else (pt1, pt2, Rp)
            x1, x2 = v3(ta[:, :shape_n], a[0]), v3(tb[:, :shape_n], a[0])
            x1 = x1.rearrange("p a (b c) -> p a b c", b=a[1])
            x2 = x2.rearrange("p a (b c) -> p a b c", b=a[1])
            kw = dict(reads=[R, rg], writes=[rg])
            kwo = dict(reads=[R, rg], writes=[rg, Rb])
            P.tt(eng, x1, a_re, b_re, ALU.mult, **kw)
            P.tt(eng, x2, a_im, b_im, ALU.mult, **kw)
            P.tt(eng, o_re, x1, x2, ALU.subtract, **kwo)
            P.tt(eng, x1, a_re, b_im, ALU.mult, **kw)
            P.tt(eng, x2, a_im, b_re, ALU.mult, **kw)
            if neg_im:
                P.stt("dve", o_im, x1, -1.0, x2, ALU.mult, ALU.subtract, **kwo)
            else:
                P.tt(eng, o_im, x1, x2, ALU.add, **kwo)

        for j0 in range(0, 32, JB):
            js = slice(j0, j0 + JB)
            tps = slice(j0 // 2, j0 // 2 + JB // 2)
            bb_r = v3(BBr, 32)[:, js, :].unsqueeze(2).to_broadcast([128, JB, 8, 16])
            bb_i = v3(BBi, 32)[:, js, :].unsqueeze(2).to_broadcast([128, JB, 8, 16])
            ex_r = E3r[:, js, 16:24].unsqueeze(3).to_broadcast([128, JB, 8, 16])
            ex_i = E3i[:, js, 16:24].unsqueeze(3).to_broadcast([128, JB, 8, 16])
            cmul_b(x4(Xr, 8), x4(Xi, 8), ex_r, ex_i, bb_r, bb_i, JB * 128, (JB, 8, 16))
            ew_r = E3r[:, js, 24:32].unsqueeze(3).to_broadcast([128, JB, 8, 16])
            ew_i = E3i[:, js, 24:32].unsqueeze(3).to_broadcast([128, JB, 8, 16])
            cmul_b(x4(Wr, 8), x4(Wi, 8), ew_r, ew_i, bb_r, bb_i, JB * 128, (JB, 8, 16))
            ey_r = E3r[:, js, 7:16].unsqueeze(3).to_broadcast([128, JB, 9, 16])
            ey_i = E3i[:, js, 7:16].unsqueeze(3).to_broadcast([128, JB, 9, 16])
            cc_r = v3(Cre, 32)[:, js, :].unsqueeze(2).to_broadcast([128, JB, 9, 16])
            cc_i = v3(Cim, 32)[:, js, :].unsqueeze(2).to_broadcast([128, JB, 9, 16])
            cmul_b(x4(Yr, 9), x4(Yi, 9), ey_r, ey_i, cc_r, cc_i, JB * 144, (JB, 9, 16), neg_im=True)
            c4 = cstg.rearrange("p (j r n) -> p j r n", j=JB, r=2)
            P.copy("dve", c4[:, :, 0, :].rearrange("p j (t c) -> p j t c", t=8), x4(Yr, 9)[:, :, 1:9, :], reads=[Rb], writes=[Rst])
            P.copy("dve", c4[:, :, 1, :].rearrange("p j (t c) -> p j t c", t=8), x4(Yi, 9)[:, :, 1:9, :], reads=[Rb], writes=[Rst])
            note(P.dma("sp", self.d["sc_wkc"][jb][:, tps, 1024:1536], cstg.rearrange("p (a n) -> p a n", a=JB // 2), ("sco", 0), reads=[Rst]))
            for jl in range(JB):
                for g2 in range(2):
                    gi = jl * 2 + g2
                    g = (j0 + jl) * 2 + g2
                    pr = slice(g2 * 64, (g2 + 1) * 64)
                    ps, psr = self.psum()
                    xr_ = x4(Xr, 8)[pr, jl].rearrange("p s c -> p (s c)")
                    xi_ = x4(Xi, 8)[pr, jl].rearrange("p s c -> p (s c)")
                    yr_ = x4(Yr, 9)[pr, jl, 0:8, :].rearrange("p t c -> p (t c)")
                    yi_ = x4(Yi, 9)[pr, jl, 0:8, :].rearrange("p t c -> p (t c)")
                    P.mm(ps[:, 0:128], xr_, yr_, True, False, reads=[Rb], writes=[psr])
                    P.mm(ps[:, 0:128], xi_, yi_, False, True, reads=[Rb], writes=[psr])
                    wr_ = x4(Wr, 8)[pr, jl].rearrange("p s c -> p (s c)")
                    wi_ = x4(Wi, 8)[pr, jl].rearrange("p s c -> p (s c)")
                    ps2, ps2r = self.psum()
                    P.mm(ps2[:, 0:64], wr_, ident[pr, pr], True, True, reads=[Rb, R], writes=[ps2r])
                    P.mm(ps2[:, 64:128], wi_, ident[pr, pr], True, True, reads=[Rb, R], writes=[ps2r])
                    kt = ktmp[gi % 2]
                    P.tt("dve", kt, ps[:, 0:128], bmask, ALU.mult, reads=[psr, R], writes=[Rst])
                    P.stt("dve", kst[:, gi * 128:(gi + 1) * 128], ident, dcol[:, g:g + 1], kt, ALU.mult, ALU.add,
                          reads=[R, Rst], writes=[Rst])
                    P.copy("act", wst[:, gi * 128:(gi + 1) * 128], ps2[:, 0:128], reads=[ps2r], writes=[Rw])
            note(P.dma("sp", self.d["sc_wkc"][jb][:, tps, 512:1024], kst.rearrange("p (a n) -> p a n", a=JB // 2), ("sco", 1), reads=[Rst]))
            note(P.dma("sp", self.d["sc_wkc"][jb][:, tps, 0:512], wst.rearrange("p (a n) -> p a n", a=JB // 2), ("sco", 3), reads=[Rw]))
            P.tt("dve", v3(targ, JB), phim[:, js].unsqueeze(2).to_broadcast([128, JB, 256]), kvec.unsqueeze(1).to_broadcast([128, JB, 256]), ALU.mult,
                 reads=[R, Rt], writes=[Rt])
            sin_of(tcs, targ, JB * 256, add=PI / 2, rg=Rt)
            sin_of(tsn, targ, JB * 256, rg=Rt)
            note(P.dma("sp", self.d["sc_tab"][jb][:, js, 0, :], v3(tcs, JB), ("sco", 2), reads=[Rt]))
            note(P.dma("sp", self.d["sc_tab"][jb][:, js, 1, :], v3(tsn, JB), ("sco", 4), reads=[Rt]))
        for r_ in (Rb, Rst, Rt, Rp, Rw):
            R.r = list(R.r) + ([r_.w] if r_.w else []) + list(r_.r)
        self.arena_regs = [R]
        self.s2_regs = [R]

    def mixer_a(self, l):
        cfg, P = self.cfg, self.P
        j = l // 2
        T, SEQ, NS, KC = cfg.T, cfg.SEQ, cfg.NS, cfg.KC
        nb = len(cfg.blocks)
        self.rmsnorm(self.vcol("mix", l))
        lnb_col = (5 * cfg.DEPTH + 1) * 8 + j * 16
        win = self.d["a_w_in"][j].rearrange("(kc p) f -> p kc f", p=128)
        wout = self.d["a_w_out"][j].rearrange("(kc p) d -> p kc d", p=128)

        s2r = self.carve2(8)
        Cb = self.s2[:, 0:4096].bitcast(F32).rearrange("p (f t) -> p f t", f=16)
        Cbr = s2r[0]
        WsT = self.s2[:, 4096:5120].rearrange("p (h t) -> p h t", h=8)
        WsTr = s2r[1]
        tmpb = [self.s2[:, 5120 + i * 1024:5120 + (i + 1) * 1024].bitcast(F32) for i in range(2)]
        tmpr = s2r[2:4]
        small = self.s2[:, 7168:8192].bitcast(F32)
        smallr = s2r[4]
        mv = small[:, 0:16].rearrange("p (t k) -> p t k", k=2)
        rr = small[:, 16:24]
        bst = small[:, 32:32 + 24]
        w0 = small[:, 64:80]
        Cbs = small[:, 80:96]
        CbS = small[:, 96:96 + 256].rearrange("p (f b) -> p f b", f=16)
        R16 = self.s2[:, 7168 + 2 * 352:7168 + 2 * 352 + 128].rearrange("p (h b) -> p h b", h=8)
        R16r = s2r[5]
        id16 = self.s2[:, 7168 + 2 * 352 + 128:7168 + 2 * 352 + 144]
        v32s = self.sq[0].rearrange("p c t -> p (c t)").bitcast(F32)
        v32r = self.sqr[0]

        P.op("pool", lambda e: e.memset(small, 0.0), writes=[smallr, R16r])
        for hh in range(2):
            t32 = tmpb[hh]
            P.dma("sp", t32.rearrange("p (h t) -> p h t", h=4), self.d["a_wsT"][j][:, hh * 4:(hh + 1) * 4, :], ("mxc", hh), writes=[tmpr[hh]])
        P.dma("sp", small[:, 128:256], self.d["trimask"][:, :], ("mxc", 2), writes=[smallr])
        for hh in range(2):
            P.tt("dve", WsT[:, hh * 4:(hh + 1) * 4, :], tmpb[hh].rearrange("p (h t) -> p h t", h=4),
                 small[:, 128:256].unsqueeze(1).to_broadcast([128, 4, 128]), ALU.mult,
                 reads=[tmpr[hh], smallr], writes=[WsTr])
        P.dma("sp", w0, self.d["a_w0"][j:j + 1, :].partition_broadcast(128), ("mxc", 3), writes=[smallr])
        for hh in range(2):
            ps, psr = self.psum()
            P.mm(ps[:, :512], self.ones[:, :], WsT[:, hh * 4:(hh + 1) * 4, :].rearrange("p h t -> p (h t)"), True, True,
                 reads=[WsTr, self.constr], writes=[psr])
            P.dma("sp", tmpb[hh], self.d["a_bs"][j:j + 1, hh * 512:(hh + 1) * 512].partition_broadcast(128), ("mxc", hh), writes=[tmpr[hh]])
            for hl in range(4):
                h = hh * 4 + hl
                for dcc in range(2):
                    fc = 2 * h + dcc
                    P.stt("dve", Cb[:, fc, :], ps[:, hl * 128:(hl + 1) * 128], self.vecs[:, lnb_col + fc:lnb_col + fc + 1],
                          tmpb[hh][:, hl * 128:(hl + 1) * 128], ALU.mult, ALU.add,
                          reads=[psr, tmpr[hh], self.constr], writes=[Cbr])
        for h in range(8):
            P.ts("dve", Cbs[:, 2 * h:2 * h + 2], self.vecs[:, lnb_col + 2 * h:lnb_col + 2 * h + 2], w0[:, h:h + 1], w0[:, 8 + h:9 + h],
                 ALU.mult, ALU.add, reads=[smallr, self.constr], writes=[smallr])
        P.copy("dve", CbS, Cbs.unsqueeze(2).to_broadcast([128, 16, 16]), reads=[smallr], writes=[smallr])
        P.dma("pool", id16[0:16, :], self.d["ident16"][:, :], ("mxc", 4), writes=[R16r])
        for h in range(8):
            P.ts("dve", R16[0:16, h, :], id16[0:16, :], w0[0:16, h:h + 1], None, ALU.mult, reads=[R16r, smallr], writes=[R16r])

        groups = [[b] for b in range(cfg.NB)]
        groups[-1].append(nb - 1)
        for grp in groups:
            cols = []
            lo = 0
            tiles = []
            for b in grp:
                t0, n = cfg.blocks[b]
                cols.append((b, t0, n, lo))
                if n == 512:
                    for i in range(4):
                        tiles.append((t0 + i * 128, 128, lo + i * 128, False))
                else:
                    tiles.append((t0, n, lo, True))
                lo += n
            NG = lo
            ntl = len(tiles)
            o = 0
            def take(n_):
                nonlocal o
                a = self.arena[:, o:o + n_]
                o += n_
                return a
            uT = take(16 * NG).rearrange("p (f t) -> p f t", f=16)
            vb = [take(2048) for _ in range(ntl)]
            gbc = take(2048)
            Wp = [take(1024).rearrange("p (h t) -> p h t", h=8) for _ in range(2)]
            assert o <= self.ARENA, (o, self.ARENA)
            regs = self.carve(16 + ntl + 1 + 2)
            ur = regs[0:16]
            vr = regs[16:16 + ntl]
            gr = regs[16 + ntl]
            Wpr = regs[16 + ntl + 1:16 + ntl + 3]
            P.dma("pool", gbc, self.d["a_ln_g"][j:j + 1, :].partition_broadcast(128), ("mxg", 0), writes=[gr])

            for p8 in range(8):
                ws, wr = self.wget(win[:, :, 2048 + p8 * 256:2048 + (p8 + 1) * 256], [KC, 256])
                for ti, (tok0, rows, lo_, is_s) in enumerate(tiles):
                    b = nb - 1 if is_s else tok0 // 512
                    ps, psr = self.psum()
                    for c in range(KC):
                        P.mm(ps[:rows, :256], self.hT[:, c, tok0:tok0 + rows], ws[:, c, :], c == 0, c == KC - 1,
                             reads=[wr, self.hr[b]], writes=[psr])
                    if is_s:
                        P.act(v32s[:rows, p8 * 256:(p8 + 1) * 256], ps[:rows, :256], AF.Gelu_apprx_tanh, reads=[psr], writes=[v32r])
                    else:
                        P.act(vb[ti][:rows, p8 * 256:(p8 + 1) * 256], ps[:rows, :256], AF.Gelu_apprx_tanh, reads=[psr], writes=[vr[ti]])
            for ti, (tok0, rows, lo_, is_s) in enumerate(tiles):
                src, srcr = (v32s, v32r) if is_s else (vb[ti], vr[ti])
                for k in range(4):
                    P.op("dve", lambda e, o_=bst[:rows, k * 6:(k + 1) * 6], i_=src[:rows, k * 512:(k + 1) * 512]: e.bn_stats(out=o_, in_=i_),
                         reads=[srcr], writes=[smallr])
                P.op("dve", lambda e, o_=mv[:rows, ti, :], i_=bst[:rows, 0:24]: e.bn_aggr(out=o_, in_=i_), reads=[smallr], writes=[smallr])
            P.act(rr[:, 0:ntl], mv[:, 0:ntl, 1], AF.Sqrt, reads=[smallr, self.constr], writes=[smallr], bias=self.epsc[:, 0:1])
            P.op("dve", lambda e, o_=rr[:, 0:ntl]: e.reciprocal(out=o_, in_=o_), reads=[smallr], writes=[smallr])
            for p8 in range(8):
                ws, wr = self.wget(win[:, :, p8 * 256:(p8 + 1) * 256], [KC, 256])
                for jj in range(2):
                    fc = 2 * p8 + jj
                    for (b, t0, n, lo_) in cols:
                        ps, psr = self.psum()
                        for c in range(KC):
                            P.mm(ps[:, :n], ws[:, c, jj * 128:(jj + 1) * 128], self.hT[:, c, t0:t0 + n], c == 0, c == KC - 1,
                                 reads=[wr, self.hr[b]], writes=[psr])
                        P.act(uT[:, fc, lo_:lo_ + n], ps[:, :n], AF.Gelu_apprx_tanh, reads=[psr], writes=[ur[fc]])
            for ti, (tok0, rows, lo_, is_s) in enumerate(tiles):
                ncols = rows
                if is_s:
                    P.stt("dve", vb[ti][:rows, :], v32s[:rows, :], mv[:rows, ti, 0:1], gbc[:rows, :], ALU.subtract, ALU.mult,
                          reads=[v32r, smallr, gr], writes=[vr[ti]])
                    P.ts("dve", v32s[:rows, :], v32s[:rows, :], mv[:rows, ti, 0:1], rr[:rows, ti:ti + 1], ALU.subtract, ALU.mult,
                         reads=[smallr], writes=[v32r])
                    for k in range(4):
                        gi = self.rot("rs", 2)
                        gt, gtr = self.rs[gi], self.rsr[gi]
                        P.dma("sp", gt[:rows, :], self.d["a_ln_g"][j:j + 1, k * 512:(k + 1) * 512].partition_broadcast(rows), ("mxs", gi), writes=[gtr])
                        P.tt("dve", v32s[:rows, k * 512:(k + 1) * 512], v32s[:rows, k * 512:(k + 1) * 512], gt[:rows, :], ALU.mult,
                             reads=[gtr], writes=[v32r])
                        gi = self.rot("rs", 2)
                        gt, gtr = self.rs[gi], self.rsr[gi]
                        P.dma("sp", gt[:rows, :], self.d["a_ln_b"][j:j + 1, k * 512:(k + 1) * 512].partition_broadcast(rows), ("mxs", gi), writes=[gtr])
                        P.tt("dve", v32s[:rows, k * 512:(k + 1) * 512], v32s[:rows, k * 512:(k + 1) * 512], gt[:rows, :], ALU.add,
                             reads=[gtr], writes=[v32r])
                    self.out_toks.append(P.dma("sp", self.d["chunkv_out"][j], v32s[:rows, :], ("cvo", 0), reads=[v32r]))
                else:
                    P.stt("dve", vb[ti][:rows, :], vb[ti][:rows, :], mv[:rows, ti, 0:1], gbc[:rows, :], ALU.subtract, ALU.mult,
                          reads=[smallr, gr], writes=[vr[ti]])
                wi = self.rot("Wp", 2)
                if is_s:
                    P.ts("dve", Wp[wi][:rows, :, :ncols], R16[:rows, :, :], rr[:rows, ti:ti + 1], None, ALU.mult,
                         reads=[R16r, smallr], writes=[Wpr[wi]])
                else:
                    P.ts("dve", Wp[wi][:, :, :], WsT[:, :, :], rr[:, ti:ti + 1], None, ALU.mult,
                         reads=[WsTr, smallr], writes=[Wpr[wi]])
                for fc4 in range(4):
                    ps, psr = self.psum()
                    for jj in range(4):
                        fc = fc4 * 4 + jj
                        P.mm(ps[:, jj * 128:jj * 128 + ncols], vb[ti][:rows, fc * 128:(fc + 1) * 128], Wp[wi][:rows, fc // 2, :ncols],
                             True, True, reads=[vr[ti], Wpr[wi]], writes=[psr])
                    bi = self.rot("tmpb", 2)
                    tb, tbr = tmpb[bi].rearrange("p (f t) -> p f t", f=4), tmpr[bi]
                    cb = CbS[:, fc4 * 4:(fc4 + 1) * 4, :] if is_s else Cb[:, fc4 * 4:(fc4 + 1) * 4, :]
                    P.tt("dve", tb[:, :, :ncols], ps[:, :].rearrange("p (f t) -> p f t", f=4)[:, :, :ncols], cb, ALU.add,
                         reads=[psr, Cbr, smallr], writes=[tbr])
                    P.tt("dve", uT[:, fc4 * 4:(fc4 + 1) * 4, lo_:lo_ + ncols], tb[:, :, :ncols], uT[:, fc4 * 4:(fc4 + 1) * 4, lo_:lo_ + ncols], ALU.mult,
                         reads=[tbr] + ur[fc4 * 4:(fc4 + 1) * 4], writes=ur[fc4 * 4:(fc4 + 1) * 4])
            for dc in range(KC):
                ws, wr = self.wget(wout[:, :, dc * 128:(dc + 1) * 128], [16, 128])
                for (b, t0, n, lo_) in cols:
                    ps, psr = self.psum()
                    for fc in range(16):
                        P.mm(ps[:, :n], ws[:, fc, :], uT[:, fc, lo_:lo_ + n], fc == 0, fc == 15, reads=[wr, ur[fc]], writes=[psr])
                    P.tt("dve", self.xT[:, dc, t0:t0 + n], ps[:, :n], self.xT[:, dc, t0:t0 + n], ALU.add,
                         reads=[psr, self.xr[b]], writes=[self.xr[b]])

    def mixer_b(self, l):
        cfg, P = self.cfg, self.P
        jb = l // 2
        T, SEQ, NS, KC = cfg.T, cfg.SEQ, cfg.NS, cfg.KC
        NK = SEQ // 8
        nb = len(cfg.blocks)
        allx = list(self.xr)
        allh = list(self.hr)
        self._rmsnorm_blocks(self.vcol("mix", l), None, lambda b: self.hr[b], perm=True)

        U = self.arena[:, 0:64 * NK].rearrange("p (g k) -> p g k", g=64)
        Sel = self.arena[:, 64 * NK:64 * NK + 8192].rearrange("p (m n) -> p m n", m=64)
        assert 64 * NK + 8192 <= self.ARENA
        regs = self.carve(65)
        Ur, Selr = regs[0:64], regs[64]
        s2r = self.carve2(4)
        Us = self.s2[:, 0:1024].rearrange("p (g b) -> p g b", g=64)
        Usr = [Reg() for _ in range(64)]
        for r_ in Usr:
            r_.r = list(s2r[0].r)
        self.s2_regs = list(self.s2_regs) + Usr
        h0 = [self.s2[:, 1024 + i * 1024:2048 + i * 1024].bitcast(F32).rearrange("p (j b) -> p j b", j=32) for i in range(2)]
        Hps = [self.s2[:, 3072 + i * 512:3584 + i * 512].rearrange("p (j b) -> p j b", j=32) for i in range(2)]
        E1 = [self.s2[:, 4096 + i * 1024:5120 + i * 1024].bitcast(F32).rearrange("p (j b) -> p j b", j=32) for i in range(2)]
        stout = self.s2[:, 6144:6272].bitcast(F32).rearrange("p (j r) -> p j r", j=32)
        tmpc = [self.s2[:, 6272 + i * 512:6784 + i * 512].bitcast(F32).rearrange("p (j b) -> p j b", j=32)[:, :, :] if False else
                self.s2[:, 6272 + i * 1024:7296 + i * 1024].bitcast(F32).rearrange("p (j b) -> p j b", j=32) for i in range(1)]
        sreg = s2r[1]
        stout_r = s2r[2]
        for i in range(4):
            P.dma("pool", Sel[:, i * 16:(i + 1) * 16, :], self.d["selmat"][:, i * 2048:(i + 1) * 2048].rearrange("p (m n) -> p m n", m=16),
                  ("sel", 0), writes=[Selr])
        P.dma("sp", h0[0], self.d["ssm_h0"][jb, 0].rearrange("p (j b) -> p j b", j=32), ("h0", 0), writes=[sreg])
        P.dma("sp", h0[1], self.d["ssm_h0"][jb, 1].rearrange("p (j b) -> p j b", j=32), ("h0", 0), writes=[sreg])
        bcb = lambda k_: self.ssmc[:, jb, k_, :].unsqueeze(2).to_broadcast([128, 32, NS])
        t_ = tmpc[0]

        def cm(o_re, o_im, kre, kim):
            P.tt("dve", t_, bcb(kre), h0[0], ALU.mult, reads=[sreg, self.ssmc_r], writes=[sreg])
            P.tt("dve", o_re, bcb(kim), h0[1], ALU.mult, reads=[sreg, self.ssmc_r], writes=[sreg])
            P.tt("dve", o_re, t_, o_re, ALU.subtract, reads=[sreg], writes=[sreg])
            P.tt("dve", t_, bcb(kre), h0[1], ALU.mult, reads=[sreg, self.ssmc_r], writes=[sreg])
            P.tt("dve", o_im, bcb(kim), h0[0], ALU.mult, reads=[sreg, self.ssmc_r], writes=[sreg])
            P.tt("dve", o_im, t_, o_im, ALU.add, reads=[sreg], writes=[sreg])

        cm(E1[0], E1[1], 1, 2)
        hp32 = [self.rs[0][:, :].rearrange("p (j b) -> p j b", j=32), self.rs[1][:, :].rearrange("p (j b) -> p j b", j=32)]
        for i in range(2):
            P.wait_tok("dve", self.rsr[i].w)
            for t in self.rsr[i].r:
                P.wait_tok("dve", t)
        cm(hp32[0], hp32[1], 3, 4)
        P.copy("dve", Hps[0], hp32[0], reads=[sreg], writes=[sreg])
        P.copy("dve", Hps[1], hp32[1], reads=[sreg], writes=[sreg])
        for i in range(2):
            self.rsr[i].w = sreg.w
            self.rsr[i].r = []

        for g in range(64):
            c, gl = g // 8, g % 8
            ps, psr = self.psum()
            for s_ in range(8):
                P.mm(ps[:, :NK], Sel[:, gl * 8 + s_, :], self.hT[:, c, s_ * NK:(s_ + 1) * NK], s_ == 0, s_ == 7,
                     reads=[Selr] + allh, writes=[psr])
            P.mm(ps[:, NK:NK + NS], Sel[:, gl * 8 + 7, :], self.hT[:, c, SEQ:SEQ + NS], True, True, reads=[Selr] + allh, writes=[psr])
            eng = "act" if g % 2 == 0 else "dve"
            P.copy(eng, U[:, g, :], ps[:, :NK], reads=[psr], writes=[Ur[g]])
            P.copy(eng, Us[:, g, :], ps[:, NK:NK + NS], reads=[psr], writes=[Usr[g]])

        HW = self.hT.rearrange("p c t -> p (c t)")
        o = 0
        def wtake(n_):
            nonlocal o
            a = HW[:, o:o + n_]
            o += n_
            return a
        N2 = 2 * NK
        f32v = lambda ap: ap.bitcast(F32).rearrange("p (i k) -> p i k", i=2)
        tabs = [[f32v(wtake(2 * N2)) for _ in range(2)] for _ in range(2)]
        zr, zi, m1, m2 = (f32v(wtake(2 * N2)) for _ in range(4))
        Hp = [[wtake(2 * (NK + NS)).rearrange("p (i k) -> p i k", i=2) for _ in range(2)] for _ in range(2)]
        assert o <= KC * T, (o, KC * T)
        inh = {}
        for r_ in allh:
            for t in ([r_.w] if r_.w else []) + list(r_.r):
                if t is not None and inh.get(t[0], 0) < t[1]:
                    inh[t[0]] = t[1]
        def wreg():
            r_ = Reg()
            r_.r = list(inh.items())
            return r_
        tabr = [wreg() for _ in range(2)]
        zreg = wreg()
        Hpr = [wreg() for _ in range(2)]
        for b_ in range(2):
            for ri in range(2):
                P.op("pool", lambda e, a_=Hp[b_][ri]: e.memset(a_, 0.0), writes=[Hpr[b_]])
        rho = self.ssmc[:, jb, 0, :]
        for k_, v_ in self.scr_toks[jb].items():
            P.wait_tok("sp", (k_, v_))
        st = {}

        def stage_dh(tp):
            wk, wkr = self.wget(self.d["sc_wkc"][jb][:, tp, :], [1536], dep=jb)
            W_ = wk[:, 0:512].rearrange("p (g n) -> p g n", g=4)
            Km = wk[:, 512:1024].rearrange("p (g n) -> p g n", g=4)
            Cc = wk[:, 1024:1536].rearrange("p (i r n) -> p i r n", i=2, r=2)
            bi = tp % 2
            cs, sn = tabs[bi]
            P.dma("sp", cs, self.d["sc_tab"][jb][:, 2 * tp:2 * tp + 2, 0, 0:NK], ("tab", bi), writes=[tabr[bi]])
            P.dma("sp", sn, self.d["sc_tab"][jb][:, 2 * tp:2 * tp + 2, 1, 0:NK], ("tab", bi), writes=[tabr[bi]])
            pre, prer = self.psum()
            pim, pimr = self.psum()
            pss, pssr = self.psum()
            for i in range(2):
                for g2 in range(2):
                    gi = 2 * i + g2
                    g = 4 * tp + gi
                    pr = slice(g2 * 64, (g2 + 1) * 64)
                    P.mm(pre[pr, i * NK:(i + 1) * NK], W_[:, gi, 0:64], U[:, g, :], True, True, reads=[wkr, Ur[g]], writes=[prer])
                    P.mm(pim[pr, i * NK:(i + 1) * NK], W_[:, gi, 64:128], U[:, g, :], True, True, reads=[wkr, Ur[g]], writes=[pimr])
                    P.mm(pss[pr, i * NS:(i + 1) * NS], W_[:, gi, 0:64], Us[:, g, :], True, True, reads=[wkr, Usr[g]], writes=[pssr])
                    P.mm(pss[pr, 32 + i * NS:32 + (i + 1) * NS], W_[:, gi, 64:128], Us[:, g, :], True, True, reads=[wkr, Usr[g]], writes=[pssr])
            st[tp] = (W_, Km, Cc, wkr, bi, cs, sn, pre, prer, pim, pimr, pss, pssr)

        def stage_dve(tp):
            (W_, Km, Cc, wkr, bi, cs, sn, pre, prer, pim, pimr, pss, pssr) = st[tp]
            dre = pre[:, 0:N2].rearrange("p (i k) -> p i k", i=2)
            dim_ = pim[:, 0:N2].rearrange("p (i k) -> p i k", i=2)
            P.tt("dve", m1, dre, cs, ALU.mult, reads=[prer, tabr[bi]], writes=[zreg])
            P.tt("dve", m2, dim_, sn, ALU.mult, reads=[pimr, tabr[bi]], writes=[zreg])
            P.tt("dve", zr, m1, m2, ALU.add, reads=[zreg], writes=[zreg])
            P.tt("dve", m1, dim_, cs, ALU.mult, reads=[pimr, tabr[bi]], writes=[zreg])
            P.tt("dve", m2, dre, sn, ALU.mult, reads=[prer, tabr[bi]], writes=[zreg])
            P.tt("dve", zi, m1, m2, ALU.subtract, reads=[zreg], writes=[zreg])
            for i in range(2):
                j = 2 * tp + i
                rb = rho[:, j:j + 1].to_broadcast([128, NK])
                P.op("dve", lambda e, o_=m1[:, i, :], d0=rb, d1=zr[:, i, :]: e.tensor_tensor_scan(out=o_, data0=d0, data1=d1, initial=0.0, op0=ALU.mult, op1=ALU.add),
                     reads=[zreg, self.ssmc_r], writes=[zreg])
                P.op("dve", lambda e, o_=m2[:, i, :], d0=rb, d1=zi[:, i, :]: e.tensor_tensor_scan(out=o_, data0=d0, data1=d1, initial=0.0, op0=ALU.mult, op1=ALU.add),
                     reads=[zreg, self.ssmc_r], writes=[zreg])
            P.tt("dve", zr, m1, cs, ALU.mult, reads=[zreg, tabr[bi]], writes=[zreg])
            P.tt("dve", zi, m2, sn, ALU.mult, reads=[zreg, tabr[bi]], writes=[zreg])
            P.tt("dve", zr, zr, zi, ALU.subtract, reads=[zreg], writes=[zreg])
            P.tt("dve", zi, m2, cs, ALU.mult, reads=[zreg, tabr[bi]], writes=[zreg])
            P.tt("dve", m1, m1, sn, ALU.mult, reads=[zreg, tabr[bi]], writes=[zreg])
            P.tt("dve", zi, zi, m1, ALU.add, reads=[zreg], writes=[zreg])
            hb = tp % 2
            Hre, Him = Hp[hb]
            P.copy("act", Hre[:, :, 1:NK], zr[:, :, 0:NK - 1], reads=[zreg], writes=[Hpr[hb]])
            P.copy("act", Him[:, :, 1:NK], zi[:, :, 0:NK - 1], reads=[zreg], writes=[Hpr[hb]])
            P.copy("act", Hre[:, :, NK:NK + NS], Hps[0][:, 2 * tp:2 * tp + 2, :], reads=[sreg], writes=[Hpr[hb]])
            P.copy("act", Him[:, :, NK:NK + NS], Hps[1][:, 2 * tp:2 * tp + 2, :], reads=[sreg], writes=[Hpr[hb]])
            P.copy("act", stout[:, 2 * tp:2 * tp + 2, 0], zr[:, :, NK - 1], reads=[zreg], writes=[stout_r])
            P.copy("act", stout[:, 2 * tp:2 * tp + 2, 1], zi[:, :, NK - 1], reads=[zreg], writes=[stout_r])
            P.tt("dve", E1[0][:, 2 * tp:2 * tp + 2, :], E1[0][:, 2 * tp:2 * tp + 2, :], pss[:, 0:32].rearrange("p (i b) -> p i b", i=2), ALU.add,
                 reads=[pssr, sreg], writes=[sreg])
            P.tt("dve", E1[1][:, 2 * tp:2 * tp + 2, :], E1[1][:, 2 * tp:2 * tp + 2, :], pss[:, 32:64].rearrange("p (i b) -> p i b", i=2), ALU.add,
                 reads=[pssr, sreg], writes=[sreg])

        def stage_y(tp):
            (W_, Km, Cc, wkr, bi, cs, sn, pre, prer, pim, pimr, pss, pssr) = st[tp]
            hb = tp % 2
            Hre, Him = Hp[hb]
            for gi in range(4):
                i, g2 = gi // 2, gi % 2
                g = 4 * tp + gi
                pr = slice(g2 * 64, (g2 + 1) * 64)
                ps, psr = self.psum()
                for (c0, c1, rhsU, rr_) in ((0, NK, U[:, g, :], Ur[g]), (NK, NK + NS, Us[:, g, :], Usr[g])):
                    P.mm(ps[:, c0:c1], Km[:, gi, :], rhsU, True, False, reads=[wkr, rr_], writes=[psr])
                    P.mm(ps[:, c0:c1], Cc[pr, i, 0, :], Hre[pr, i, c0:c1], False, False, reads=[wkr, Hpr[hb]], writes=[psr])
                    P.mm(ps[:, c0:c1], Cc[pr, i, 1, :], Him[pr, i, c0:c1], False, True, reads=[wkr, Hpr[hb]], writes=[psr])
                P.act(U[:, g, :], ps[:, 0:NK], AF.Gelu_apprx_tanh, reads=[psr], writes=[Ur[g]])
                P.act(Us[:, g, :], ps[:, NK:NK + NS], AF.Gelu_apprx_tanh, reads=[psr], writes=[Usr[g]])

        stage_dh(0)
        for tp in range(16):
            stage_dve(tp)
            if tp + 1 < 16:
                stage_dh(tp + 1)
            stage_y(tp)
        self.out_toks.append(P.dma("sp", self.d["ssm_pr_out"][jb], stout, ("sso", 0), reads=[stout_r]))
        self.out_toks.append(P.dma("sp", self.d["ssm_s_out"][jb, 0].rearrange("p (j b) -> p j b", j=32), E1[0], ("sso", 0), reads=[sreg]))
        self.out_toks.append(P.dma("sp", self.d["ssm_s_out"][jb, 1].rearrange("p (j b) -> p j b", j=32), E1[1], ("sso", 0), reads=[sreg]))

        inh2 = {}
        for r_ in tabr + [zreg] + Hpr:
            for t in ([r_.w] if r_.w else []) + list(r_.r):
                if t is not None and inh2.get(t[0], 0) < t[1]:
                    inh2[t[0]] = t[1]
        for r_ in allh:
            r_.r = list(r_.r) + list(inh2.items())
        for c in range(KC):
            for s_ in range(8):
                ps, psr = self.psum()
                for gl in range(8):
                    P.mm(ps[:, :NK], Sel[:, s_ * 8 + gl, :], U[:, 8 * c + gl, :], gl == 0, gl == 7, reads=[Selr, Ur[8 * c + gl]], writes=[psr])
                eng = "act" if s_ % 2 == 0 else "dve"
                P.copy(eng, self.hT[:, c, s_ * NK:(s_ + 1) * NK], ps[:, :NK], reads=[psr], writes=allh)
            ps, psr = self.psum()
            for gl in range(8):
                P.mm(ps[:, :NS], Sel[:, 7 * 8 + gl, :], Us[:, 8 * c + gl, :], gl == 0, gl == 7, reads=[Selr, Usr[8 * c + gl]], writes=[psr])
            P.copy("dve", self.hT[:, c, SEQ:SEQ + NS], ps[:, :NS], reads=[psr], writes=allh)

        wglu = self.d["b_w_glu"][jb].rearrange("(kc p) n -> p kc n", p=128)
        sgr = self.carve(3)
        sgb = [self.arena[:, i * 512:(i + 1) * 512] for i in range(3)]
        pblocks = [(q0, 512) for q0 in range(0, SEQ, 512)] + [(SEQ, NS)]
        for d2 in range(KC // 2):
            wv_, wvr = self.wget(wglu[:, :, d2 * 256:(d2 + 1) * 256], [KC, 256])
            wg_, wgr = self.wget(wglu[:, :, 1024 + d2 * 256:1024 + (d2 + 1) * 256], [KC, 256])
            for jj in range(2):
                dc = d2 * 2 + jj
                for (q0, n) in pblocks:
                    pv, pvr = self.psum()
                    pg, pgr = self.psum()
                    for c in range(KC):
                        P.mm(pv[:, :n], wv_[:, c, jj * 128:(jj + 1) * 128], self.hT[:, c, q0:q0 + n], c == 0, c == KC - 1, reads=[wvr] + allh, writes=[pvr])
                    for c in range(KC):
                        P.mm(pg[:, :n], wg_[:, c, jj * 128:(jj + 1) * 128], self.hT[:, c, q0:q0 + n], c == 0, c == KC - 1, reads=[wgr] + allh, writes=[pgr])
                    si = self.rot("sgb", 3)
                    sg32 = sgb[si].bitcast(F32)
                    P.act(sgb[si][:, :n], pg[:, :n], AF.Sigmoid, reads=[pgr], writes=[sgr[si]])
                    if n == 512:
                        ns_ = 512 // NK
                        s0 = q0 // NK
                        xv = self.xT[:, dc, 0:SEQ].rearrange("p (k s) -> p s k", s=8)[:, s0:s0 + ns_, :]
                        pvv = pv[:, :n].rearrange("p (s k) -> p s k", s=ns_)
                        sgv = sgb[si][:, :n].rearrange("p (s k) -> p s k", s=ns_)
                    else:
                        xv = self.xT[:, dc, q0:q0 + n]
                        pvv = pv[:, :n]
                        sgv = sgb[si][:, :n]
                    P.tt("dve", sgv, pvv, sgv, ALU.mult, reads=[pvr, sgr[si]], writes=[sgr[si]])
                    P.tt("dve", xv, xv, sgv, ALU.add, reads=[sgr[si]] + allx, writes=allx)

    def xattn(self, l):
        cfg, P = self.cfg, self.P
        T, SEQ, NS, NM, KC = cfg.T, cfg.SEQ, cfg.NS, cfg.NMEM, cfg.KC
        nb = len(cfg.blocks)
        o = 0
        def take(n):
            nonlocal o
            a = self.arena[:, o:o + n]
            o += n
            return a
        qT = take(KC * T).rearrange("p (c t) -> p c t", c=KC)
        Eb = [take(1024).rearrange("p (m t) -> p m t", m=2) for _ in range(2)]
        kT = take(KC * NM).rearrange("p (c m) -> p c m", c=KC)
        V = take(2 * 1024).rearrange("p (m f) -> p m f", m=2)
        mnT = take(KC * NM).rearrange("p (c m) -> p c m", c=KC)
        Es = self.s2[:, 7424:7552]
        assert o <= self.ARENA, (o, self.ARENA)
        regs = self.carve(nb + 2 + 1 + 1 + 1 + 1)
        qr = regs[0:nb]
        Er = regs[nb:nb + 2]
        kTr, Vr, mnr, Esr = regs[nb + 2], regs[nb + 3], regs[nb + 4], regs[nb + 5]
        s2r = self.carve2(1 + 2 + 2 + 1 + 1)
        Esr = s2r[6]
        memT = self.s2[:, 0:4096].bitcast(F32).rearrange("p (c m) -> p c m", c=KC)
        memr = s2r[0]
        rd = [self.s2[:, 4096 + i * 1024:4096 + (i + 1) * 1024].bitcast(F32) for i in range(2)]
        rdr = s2r[1:3]
        ost = [self.s2[:, 6144 + i * 512:6144 + (i + 1) * 512].bitcast(F32) for i in range(2)]
        ostr = s2r[3:5]
        rds = self.s2[:, 7168:7168 + 128].bitcast(F32)
        rdsr = s2r[5]
        gm = self.vcol("mem", l)

        P.dma("sp", memT, self.d["memT"].rearrange("(c p) m -> p c m", p=128), ("mem", 0), writes=[memr])
        for c in range(KC):
            P.stt("dve", mnT[:, c, :], memT[:, c, :], self.vecs[:, gm + c:gm + c + 1], self.mrstd[:, :],
                  ALU.mult, ALU.mult, reads=[memr, self.mrstd_r, self.constr], writes=[mnr])
        wk = self.d["x_wk"][l].rearrange("(kc p) d -> p kc d", p=128)
        wv = self.d["x_wv"][l].rearrange("(kc p) d -> p kc d", p=128)
        wq = self.d["x_wq"][l].rearrange("(kc p) d -> p kc d", p=128)
        wo = self.d["x_wo"][l].rearrange("(kc p) d -> p kc d", p=128)
        kout = self.d["memk_out"][l].rearrange("(c p) m -> p c m", p=128)
        vout = self.d["memv_out"][l].rearrange("(mc p) f -> p mc f", p=128)
        for d2 in range(KC // 2):
            ws, wr = self.wget(wk[:, :, d2 * 256:(d2 + 1) * 256], [KC, 256])
            for j in range(2):
                dc = d2 * 2 + j
                ps, psr = self.psum()
                for c in range(KC):
                    P.mm(ps[:, :NM], ws[:, c, j * 128:(j + 1) * 128], mnT[:, c, :], c == 0, c == KC - 1,
                         reads=[wr, mnr], writes=[psr])
                i = self.rot("ost", 2)
                P.copy("dve", ost[i][:, :NM], ps[:, :NM], reads=[psr], writes=[ostr[i]])
                P.copy("act", kT[:, dc, :], ost[i][:, :NM], reads=[ostr[i]], writes=[kTr])
                self.out_toks.append(P.dma("sp", kout[:, dc, :], ost[i][:, :NM], ("ost", i), reads=[ostr[i]]))
        for n4 in range(4):
            ws, wr = self.wget(wv[:, :, n4 * 256:(n4 + 1) * 256], [KC, 256])
            for mc in range(2):
                ps, psr = self.psum()
                for c in range(KC):
                    P.mm(ps[:, :256], mnT[:, c, mc * 128:(mc + 1) * 128], ws[:, c, :], c == 0, c == KC - 1,
                         reads=[wr, mnr], writes=[psr])
                i = self.rot("ost", 2)
                P.copy("dve", ost[i][:, :256], ps[:, :256], reads=[psr], writes=[ostr[i]])
                P.copy("act", V[:, mc, n4 * 256:(n4 + 1) * 256], ost[i][:, :256], reads=[ostr[i]], writes=[Vr])
                self.out_toks.append(P.dma("sp", vout[:, mc, n4 * 256:(n4 + 1) * 256], ost[i][:, :256], ("ost", i), reads=[ostr[i]]))

        self.rmsnorm(self.vcol("x", l))
        for d2 in range(KC // 2):
            ws, wr = self.wget(wq[:, :, d2 * 256:(d2 + 1) * 256], [KC, 256])
            for j in range(2):
                dc = d2 * 2 + j
                for b, (t0, n) in enumerate(cfg.blocks):
                    ps, psr = self.psum()
                    for c in range(KC):
                        P.mm(ps[:, :n], ws[:, c, j * 128:(j + 1) * 128], self.hT[:, c, t0:t0 + n], c == 0, c == KC - 1,
                             reads=[wr, self.hr[b]], writes=[psr])
                    P.op("act", lambda e, o_=qT[:, dc, t0:t0 + n], i_=ps[:, :n]: e.mul(out=o_, in_=i_, mul=0.0625),
                         reads=[psr], writes=[qr[b]])

        NE = 4
        Esb = [Es[:, i * 8:(i + 1) * 8] for i in range(NE)]
        Esr_ = [Reg() for _ in range(NE)]
        for r_ in Esr_:
            r_.r = list(Esr.r)
        rdsb = [rds[:, i * 4:(i + 1) * 4] for i in range(NE)]
        rdsr_ = [Reg() for _ in range(NE)]
        for r_ in rdsr_:
            r_.r = list(rdsr.r)
        self.s2_regs = list(self.s2_regs) + Esr_ + rdsr_
        ckT = self.d["cacheKT"][l]
        cV = self.d["cacheV"][l]

        def sample_S(s):
            ks, kr = self.wget(ckT[s].rearrange("(c p) m -> p c m", p=128), [KC, NM])
            ps, psr = self.psum()
            for mc in range(2):
                for h in range(4):
                    col = mc * 4 + h
                    for dcc in range(2):
                        P.mm(ps[:, col:col + 1], ks[:, 2 * h + dcc, mc * 128:(mc + 1) * 128],
                             qT[:, 2 * h + dcc, SEQ + s:SEQ + s + 1], dcc == 0, dcc == 1,
                             reads=[kr, qr[nb - 1]], writes=[psr])
            ei = s % NE
            P.act(Esb[ei], ps[:, 0:8], AF.Exp, reads=[psr], writes=[Esr_[ei]])

        def sample_PV(s):
            ei = s % NE
            vs, vr = self.wget(cV[s].rearrange("(mc p) f -> p mc f", p=128), [2, 1024])
            ps, psr = self.psum()
            for hdc in range(KC):
                h = hdc // 2
                for mc in range(2):
                    P.mm(ps[:, hdc:hdc + 1], vs[:, mc, hdc * 128:(hdc + 1) * 128],
                         Esb[ei][:, mc * 4 + h:mc * 4 + h + 1], mc == 0, mc == 1, reads=[vr, Esr_[ei]], writes=[psr])
            for mc in range(2):
                P.mm(ps[:, 8:12], self.ones[:, :], Esb[ei][:, mc * 4:(mc + 1) * 4], mc == 0, mc == 1,
                     reads=[Esr_[ei], self.constr], writes=[psr])
            P.op("dve", lambda e, o_=rdsb[ei], i_=ps[:, 8:12]: e.reciprocal(out=o_, in_=i_), reads=[psr], writes=[rdsr_[ei]])
            P.tt("dve", self.hT[:, :, SEQ + s].rearrange("p (h d) -> p h d", h=4),
                 ps[:, 0:8].rearrange("p (h d) -> p h d", h=4),
                 rdsb[ei].unsqueeze(2).to_broadcast([128, 4, 2]), ALU.mult,
                 reads=[psr, rdsr_[ei]], writes=[self.hr[nb - 1]])

        def prompt_block(b):
            t0, n = cfg.blocks[b]
            stE = {}

            def stage1(h):
                ei = self.rot("E", 2)
                E, Er_ = Eb[ei], Er[ei]
                for mc in range(2):
                    ps, psr = self.psum()
                    for dcc in range(2):
                        P.mm(ps[:, :n], kT[:, 2 * h + dcc, mc * 128:(mc + 1) * 128], qT[:, 2 * h + dcc, t0:t0 + n],
                             dcc == 0, dcc == 1, reads=[kTr, qr[b]], writes=[psr])
                    P.act(E[:, mc, :n], ps[:, :n], AF.Exp, reads=[psr], writes=[Er_])
                stE[h] = (E, Er_)

            def stage2(h):
                E, Er_ = stE[h]
                ps, psr = self.psum()
                for mc in range(2):
                    P.mm(ps[:, :n], self.ones[:, :], E[:, mc, :n], mc == 0, mc == 1, reads=[Er_, self.constr], writes=[psr])
                ri = self.rot("rd", 2)
                P.op("dve", lambda e, o_=rd[ri][:, :n], i_=ps[:, :n]: e.reciprocal(out=o_, in_=i_), reads=[psr], writes=[rdr[ri]])
                for dcc in range(2):
                    ps, psr = self.psum()
                    for mc in range(2):
                        P.mm(ps[:, :n], V[:, mc, (2 * h + dcc) * 128:(2 * h + dcc + 1) * 128], E[:, mc, :n],
                             mc == 0, mc == 1, reads=[Vr, Er_], writes=[psr])
                    P.tt("dve", self.hT[:, 2 * h + dcc, t0:t0 + n], ps[:, :n], rd[ri][:, :n], ALU.mult,
                         reads=[psr, rdr[ri]], writes=[self.hr[b]])

            stage1(0)
            for h in range(4):
                if h + 1 < 4:
                    stage1(h + 1)
                stage2(h)

        order = []
        per = (NS + cfg.NB - 1) // max(cfg.NB, 1)
        sdone = 0
        for b in range(cfg.NB):
            order.append(("p", b))
            for s in range(sdone, min(NS, sdone + per)):
                order.append(("s", s))
            sdone = min(NS, sdone + per)
        for s in range(sdone, NS):
            order.append(("s", s))
        prev_s = None
        for kind, i in order:
            if kind == "p":
                prompt_block(i)
            else:
                sample_S(i)
                if prev_s is not None:
                    sample_PV(prev_s)
                prev_s = i
        sample_PV(prev_s)
        for d2 in range(KC // 2):
            ws, wr = self.wget(wo[:, :, d2 * 256:(d2 + 1) * 256], [KC, 256])
            for j in range(2):
                dc = d2 * 2 + j
                for b, (t0, n) in enumerate(cfg.blocks):
                    ps, psr = self.psum()
                    for c in range(KC):
                        P.mm(ps[:, :n], ws[:, c, j * 128:(j + 1) * 128], self.hT[:, c, t0:t0 + n], c == 0, c == KC - 1,
                             reads=[wr, self.hr[b]], writes=[psr])
                    P.tt("dve", self.xT[:, dc, t0:t0 + n], ps[:, :n], self.xT[:, dc, t0:t0 + n], ALU.add,
                         reads=[psr, self.xr[b]], writes=[self.xr[b]])

    def epilogue(self):
        cfg, P = self.cfg, self.P
        gcol = self.vcol("final", 0)
        yT_d = self.d["yT"].rearrange("(c p) t -> p c t", p=128)
        nb = len(cfg.blocks)
        assert cfg.KC * 512 * 2 * 2 <= self.ARENA
        yr_all = self.carve(2)
        stages = [self.arena[:, i * cfg.KC * 1024:(i + 1) * cfg.KC * 1024].bitcast(F32).rearrange("p (c t) -> p c t", c=cfg.KC)
                  for i in range(2)]
        toks = []
        out_view = {}
        class _V:
            def __init__(s2, b):
                s2.b = b
        self._rmsnorm_blocks(gcol, lambda b, c, t0, n: stages[b % 2][:, c, 0:n], lambda b: yr_all[b % 2],
                             after=lambda b, t0, n: toks.append(
                                 P.dma("sp", yT_d[:, :, t0:t0 + n], stages[b % 2][:, :, 0:n], ("y", b % 2), reads=[yr_all[b % 2]])))
        return toks

    def build_body(self):
        cfg = self.cfg
        self.prologue()
        stages = getattr(cfg, "stages", ("ffn1", "mixer", "xattn", "ffn2"))
        if "mixer" in stages or "ssmsetup" in stages:
            for jb in range(self.NBL):
                self.ssm_setup(jb)
        for l in range(cfg.DEPTH):
            if "ffn1" in stages:
                self.ffn("ffn1", l, self.vcol("ffn1", l))
            if "mixer" in stages:
                if l % 2 == 0:
                    self.mixer_a(l)
                elif hasattr(self, "mixer_b"):
                    self.mixer_b(l)
            if "xattn" in stages:
                self.xattn(l)
            if "ffn2" in stages:
                self.ffn("ffn2", l, self.vcol("ffn2", l))
        return self.epilogue()


def build_program(cfg):
    from contextlib import ExitStack
    pieces = None
    for pass_i in range(2):
        nc = bass.Bass("TRN2", target_bir_lowering=False)
        with ExitStack() as stack:
            B = Builder(cfg, nc, dry_pieces=pieces)
            B.declare(stack)
            out_toks = B.build_body()
            if pass_i == 0:
                pieces = B.pieces
                continue
            P = B.P
            final = {}
            for t in list(out_toks) + list(B.out_toks):
                if t is not None and final.get(t[0], 0) < t[1]:
                    final[t[0]] = t[1]
            for k_, v_ in final.items():
                P.wait_tok("sp", (k_, v_))
            sems = {}
            for k in P.semkeys:
                sems[k] = stack.enter_context(nc.semaphore("s_" + "_".join(str(x) for x in (k if not isinstance(k[1], tuple) else (k[0],) + k[1]))))
            P.emit(nc, sems)
        return nc, B


def pack_vecs(cfg, inputs):
    L = cfg.DEPTH
    cols = []
    for nm in ("norm_ffn1", "norm_mix", "norm_x", "norm_ffn2", "norm_mem"):
        for l in range(L):
            cols.append(np.asarray(inputs[nm][l], np.float32).reshape(8, 128).T)
    cols.append(np.asarray(inputs["norm_final"], np.float32).reshape(8, 128).T)
    for j in range((L + 1) // 2):
        cols.append(np.asarray(inputs["a_ln_b"][j], np.float32).reshape(16, 128).T)
    return np.ascontiguousarray(np.concatenate(cols, axis=1))


def make_in_maps(cfg, inputs, n_cores):
    ns = cfg.NS
    vecs = pack_vecs(cfg, inputs)
    shared = {"vecs": vecs}
    for nm in ("x_wq", "x_wk", "x_wv", "x_wo"):
        shared[nm] = np.ascontiguousarray(inputs[nm][:cfg.DEPTH], dtype=np.float32)
    for nm in ("ffn1", "ffn2"):
        for s in ("_wg", "_wu", "_wd"):
            shared[nm + s] = np.ascontiguousarray(inputs[nm + s][:cfg.DEPTH], dtype=np.float32)
    NA = (cfg.DEPTH + 1) // 2
    shared["a_w_in"] = np.ascontiguousarray(inputs["a_w_in"][:NA], dtype=np.float32)
    shared["a_w_out"] = np.ascontiguousarray(inputs["a_w_out"][:NA], dtype=np.float32)
    shared["a_ln_g"] = np.ascontiguousarray(inputs["a_ln_g"][:NA], dtype=np.float32)
    shared["a_ln_b"] = np.ascontiguousarray(inputs["a_ln_b"][:NA], dtype=np.float32)
    ws = np.asarray(inputs["a_w_s"][:NA], np.float32)
    shared["a_wsT"] = np.ascontiguousarray(ws.transpose(0, 3, 1, 2))
    shared["a_bs"] = np.ascontiguousarray(np.asarray(inputs["a_b_s"][:NA], np.float32).reshape(NA, 1024))
    shared["a_w0"] = np.ascontiguousarray(np.concatenate([ws[:, :, 0, 0], np.asarray(inputs["a_b_s"][:NA], np.float32)[:, :, 0]], axis=1))
    NBL = max(cfg.DEPTH // 2, 1)
    f32 = lambda a: np.asarray(a, np.float32)
    gp = lambda a: f32(a)[:NBL].reshape(NBL, 32, 2, 64).transpose(0, 2, 3, 1).reshape(NBL, 128, 32)
    ldt = np.repeat(f32(inputs["b_log_dt"])[:NBL, :, None], 64, axis=2)
    shared["ssm_small"] = np.ascontiguousarray(np.stack([gp(inputs["b_a_re"]), gp(inputs["b_a_im"]), gp(ldt)], axis=2))
    gpc = lambda a: f32(a)[:NBL].reshape(NBL, 32, 2, 64, 16).transpose(0, 2, 3, 1, 4).reshape(NBL, 128, 512)
    shared["ssm_B"] = np.ascontiguousarray(np.stack([gpc(inputs["b_b_re"]), gpc(inputs["b_b_im"])], axis=1))
    gcp = lambda a: gpc(f32(a)[:NBL].transpose(0, 1, 3, 2))
    shared["ssm_C"] = np.ascontiguousarray(np.stack([gcp(inputs["b_c_re"]), gcp(inputs["b_c_im"])], axis=1))
    dcol = np.tile(f32(inputs["b_d"])[:NBL].transpose(0, 2, 1)[:, None, :, :], (1, 8, 1, 1)).reshape(NBL, 128, 64)
    shared["ssm_Dcol"] = np.ascontiguousarray(dcol)
    shared["b_w_glu"] = np.ascontiguousarray(f32(inputs["b_w_glu"])[:NBL])
    sidx = np.arange(128) // 16
    shared["blockmask"] = (sidx[None, :] >= sidx[:, None]).astype(np.float32)
    shared["ident128"] = np.eye(128, dtype=np.float32)
    shared["kvec"] = np.tile(np.arange(256, dtype=np.float32)[None, :], (128, 1))
    nvals = np.concatenate([np.arange(-7, 9), -np.arange(0, 8), np.arange(7, -1, -1)]).astype(np.float32)
    shared["nvec"] = np.tile(nvals[None, :], (128, 1))
    sel = np.zeros((128, 8, 8, 128), np.float32)
    eye = np.eye(128, dtype=np.float32)
    for a_ in range(8):
        for b_ in range(8):
            sel[:, a_, b_, 16 * b_:16 * b_ + 16] = eye[:, 16 * a_:16 * a_ + 16]
    shared["selmat"] = np.ascontiguousarray(sel.reshape(128, 64 * 128))
    shared["trimask"] = np.triu(np.ones((128, 128), np.float32))
    shared["ident16"] = np.eye(16, dtype=np.float32)
    maps = []
    for c in range(n_cores):
        xp = np.asarray(inputs["x_prompt"][c, :cfg.SEQ], np.float32)
        xs = np.asarray(inputs["x_sample"][c * ns:(c + 1) * ns, 0], np.float32)
        xT = np.ascontiguousarray(np.concatenate([xp, xs], axis=0).T)
        m = dict(shared)
        m["xT"] = xT
        m["memT"] = np.ascontiguousarray(np.asarray(inputs["mem_prompt"][c], np.float32).T)
        ck = np.asarray(inputs["cache_mem_k"][:cfg.DEPTH, c * ns:(c + 1) * ns], np.float32).reshape(cfg.DEPTH, ns, cfg.NMEM, cfg.D)
        m["cacheKT"] = np.ascontiguousarray(ck.transpose(0, 1, 3, 2))
        m["cacheV"] = np.ascontiguousarray(np.asarray(inputs["cache_mem_v"][:cfg.DEPTH, c * ns:(c + 1) * ns], np.float32).reshape(cfg.DEPTH, ns, cfg.NMEM, cfg.D))
        h0 = lambda a: f32(a)[:NBL, c * ns:(c + 1) * ns].reshape(NBL, ns, 32, 2, 64).transpose(0, 3, 4, 2, 1).reshape(NBL, 128, 32 * ns)
        m["ssm_h0"] = np.ascontiguousarray(np.stack([h0(inputs["state_ssm_re"]), h0(inputs["state_ssm_im"])], axis=1))
        maps.append(m)
    return maps


def kernel(**inputs):
    cfg = Cfg()
    n = 8
    nc, B = build_program(cfg)
    in_maps = make_in_maps(cfg, inputs, n)
    res = run_bass_kernel_spmd(nc, in_maps, core_ids=list(range(n)))
    R_ = res.results
    L, NS, SEQ = cfg.DEPTH, cfg.NS, cfg.SEQ
    f32 = np.float32
    ys = [r["yT"] for r in R_]
    y_prompt = np.stack([y[:, :SEQ].T for y in ys]).astype(f32)
    y_sample = np.concatenate([y[:, SEQ:].T for y in ys])[:, None, :].astype(f32)
    mem_k = np.stack([r["memk_out"].transpose(0, 2, 1).reshape(L, 256, 4, 256) for r in R_], axis=1).astype(f32)
    mem_v = np.stack([r["memv_out"].reshape(L, 256, 4, 256) for r in R_], axis=1).astype(f32)
    NBL = L // 2
    gp = lambda a: a.reshape(2, 64, 32).transpose(2, 0, 1).reshape(64, 64)
    gs = lambda a: a.reshape(2, 64, 32, NS).transpose(3, 2, 0, 1).reshape(NS, 64, 64)
    ssm_re_p = np.stack([np.stack([gp(r["ssm_pr_out"][j, :, :, 0]) for r in R_]) for j in range(NBL)]).astype(f32)
    ssm_im_p = np.stack([np.stack([gp(r["ssm_pr_out"][j, :, :, 1]) for r in R_]) for j in range(NBL)]).astype(f32)
    ssm_re_s = np.stack([np.concatenate([gs(r["ssm_s_out"][j, 0]) for r in R_]) for j in range(NBL)]).astype(f32)
    ssm_im_s = np.stack([np.concatenate([gs(r["ssm_s_out"][j, 1]) for r in R_]) for j in range(NBL)]).astype(f32)
    chunk_v = np.concatenate([r["chunkv_out"] for r in R_], axis=1)[:, :, None, :].astype(f32)
    return (y_prompt, y_sample, mem_k, mem_v, ssm_re_p, ssm_im_p, ssm_re_s, ssm_im_s, chunk_v)
```
